# Optimizing a Trainium2 kernel written in Bass

```python
import jax, jax.numpy as jnp
from jax import lax
import numpy as np

D_MODEL = 2048
BATCH = 4
SEQ = 2048
DEPTH = 1
DEC_BATCH = 128
DEC_SEQ = 8
PAST_LEN = 8192
PAGE_SIZE = 128

N_META = 16
HEAD_DIM = 64
ATTN_WIDTH = D_MODEL // 2
CONV_CH = D_MODEL - ATTN_WIDTH
N_HEADS = ATTN_WIDTH // HEAD_DIM
N_KV_HEADS = 4
GQA_GROUP = N_HEADS // N_KV_HEADS
KV_WIDTH = N_KV_HEADS * HEAD_DIM
IN_COLS = ATTN_WIDTH + 2 * KV_WIDTH + 2 * CONV_CH
WINDOW = 128
BLOCK = 128
ROPE_THETA = 500000.0
ROT_DIM = HEAD_DIM // 4
CONV_K = 31
D_FF = 4 * D_MODEL
RMS_EPS = 1e-6
LN_EPS = 1e-5
MASK_VALUE = -1e30

kernel_name = 'hybrid_swa_sink_conformer_conv_decoder'


def rms_norm(x, g):
    xf = x.astype(jnp.float32)
    y = xf * lax.rsqrt(jnp.mean(xf * xf, axis=-1, keepdims=True) + RMS_EPS)
    return (y * g.astype(jnp.float32)).astype(x.dtype)


def partial_rope(x, pos):
    half = ROT_DIM // 2
    inv = jnp.power(jnp.float32(ROPE_THETA), -jnp.arange(half, dtype=jnp.float32) * 2.0 / ROT_DIM)
    ang = pos.astype(jnp.float32)[:, None] * inv[None, :]
    cos = jnp.cos(ang)[:, None, :]
    sin = jnp.sin(ang)[:, None, :]
    xr = x[..., :ROT_DIM].astype(jnp.float32)
    x1, x2 = xr[..., :half], xr[..., half:]
    rot = jnp.concatenate([x1 * cos - x2 * sin, x2 * cos + x1 * sin], axis=-1)
    return jnp.concatenate([rot.astype(x.dtype), x[..., ROT_DIM:]], axis=-1)


def mixer_inputs(h, g, w_in):
    xn = rms_norm(h, g)
    z = jnp.einsum('btd,dc->btc', xn, w_in)
    q, k, v, a, b = jnp.split(z, [ATTN_WIDTH, ATTN_WIDTH + KV_WIDTH, ATTN_WIDTH + 2 * KV_WIDTH,
                                  ATTN_WIDTH + 2 * KV_WIDTH + CONV_CH], axis=-1)
    lead = h.shape[:-1]
    q = q.reshape(lead + (N_HEADS, HEAD_DIM))
    k = k.reshape(lead + (N_KV_HEADS, HEAD_DIM))
    v = v.reshape(lead + (N_KV_HEADS, HEAD_DIM))
    u = a * jax.nn.sigmoid(b)
    return q, k, v, u


def sink_probs(s, mask, sink):
    s = jnp.where(mask, s, MASK_VALUE)
    m = jnp.maximum(jnp.max(s, axis=-1, keepdims=True), sink)
    e = jnp.exp(s - m)
    return e / (jnp.sum(e, axis=-1, keepdims=True) + jnp.exp(sink - m))


def swa_prompt(q, k, v, sinks):
    n = q.shape[0]
    pad = (-N_META) % BLOCK
    padf = lambda t: jnp.pad(t, ((0, 0), (pad, 0), (0, 0), (0, 0)))
    lp = q.shape[1] + pad
    nb = lp // BLOCK
    qb = padf(q).reshape(n, nb, BLOCK, N_KV_HEADS, GQA_GROUP, HEAD_DIM)
    kb = padf(k).reshape(n, nb, BLOCK, N_KV_HEADS, HEAD_DIM)
    vb = padf(v).reshape(n, nb, BLOCK, N_KV_HEADS, HEAD_DIM)
    prev = lambda t: jnp.pad(t, ((0, 0), (1, 0), (0, 0), (0, 0), (0, 0)))[:, :-1]
    kk = jnp.concatenate([prev(kb), kb], axis=2)
    vv = jnp.concatenate([prev(vb), vb], axis=2)
    qpos = (jnp.arange(lp, dtype=jnp.int32) - pad).reshape(nb, BLOCK)
    kpos = jnp.concatenate([qpos - BLOCK, qpos], axis=1)
    diff = qpos[:, :, None] - kpos[:, None, :]
    mask = (diff >= 0) & (diff < WINDOW) & (kpos[:, None, :] >= 0)
    s = jnp.einsum('bnqkgd,bnskd->bnkgqs', qb, kk, preferred_element_type=jnp.float32)
    s = s * (HEAD_DIM ** -0.5)
    sink = sinks.astype(jnp.float32).reshape(N_KV_HEADS, GQA_GROUP)[None, None, :, :, None, None]
    p = sink_probs(s, mask[None, :, None, None], sink)
    o = jnp.einsum('bnkgqs,bnskd->bnqkgd', p.astype(v.dtype), vv)
    return o.reshape(n, lp, ATTN_WIDTH)[:, pad:]


def swa_sample(q, kk, vv, sinks):
    n, t = q.shape[:2]
    w = kk.shape[1] - t
    qpos = PAST_LEN + jnp.arange(t, dtype=jnp.int32)
    kpos = PAST_LEN - w + jnp.arange(w + t, dtype=jnp.int32)
    diff = qpos[:, None] - kpos[None, :]
    mask = (diff >= 0) & (diff < WINDOW)
    qg = q.reshape(n, t, N_KV_HEADS, GQA_GROUP, HEAD_DIM)
    s = jnp.einsum('bqkgd,bskd->bkgqs', qg, kk, preferred_element_type=jnp.float32)
    s = s * (HEAD_DIM ** -0.5)
    sink = sinks.astype(jnp.float32).reshape(N_KV_HEADS, GQA_GROUP)[None, :, :, None, None]
    p = sink_probs(s, mask[None, None, None], sink)
    o = jnp.einsum('bkgqs,bskd->bqkgd', p.astype(vv.dtype), vv)
    return o.reshape(n, t, ATTN_WIDTH)


def conv_branch(u_hist, w_dw, b_dw, ln_g, ln_b):
    y = lax.conv_general_dilated(u_hist, w_dw[:, None, :].astype(u_hist.dtype), window_strides=(1,),
                                 padding='VALID', dimension_numbers=('NWC', 'WIO', 'NWC'),
                                 feature_group_count=CONV_CH)
    yf = (y + b_dw).astype(jnp.float32)
    mu = jnp.mean(yf, axis=-1, keepdims=True)
    var = jnp.mean(jnp.square(yf - mu), axis=-1, keepdims=True)
    yn = (yf - mu) * lax.rsqrt(var + LN_EPS) * ln_g.astype(jnp.float32) + ln_b.astype(jnp.float32)
    return jax.nn.silu(yn).astype(u_hist.dtype)


def merge_groups(attn, conv, w_out):
    return jnp.einsum('btc,cd->btd', jnp.concatenate([attn, conv], axis=-1), w_out)


def sq_relu_mlp(h, g, w_up, w_down):
    a = jax.nn.relu(jnp.einsum('btd,df->btf', rms_norm(h, g), w_up))
    return jnp.einsum('btf,fd->btd', a * a, w_down)


def setup_inputs(seed: int = 0) -> dict:
    key = jax.random.key(seed)
    ks = jax.random.split(key, 20)
    f32 = jnp.float32
    nrm = lambda k, shape, scale: jax.random.normal(k, shape, f32) * scale
    return {
        'x_prompt': nrm(ks[0], (BATCH, SEQ, D_MODEL), 1.0),
        'x_sample': nrm(ks[1], (DEC_BATCH, DEC_SEQ, D_MODEL), 1.0),
        'cache_k': nrm(ks[2], (DEPTH, DEC_BATCH, WINDOW, N_KV_HEADS, HEAD_DIM), 1.0),
        'cache_v': nrm(ks[3], (DEPTH, DEC_BATCH, WINDOW, N_KV_HEADS, HEAD_DIM), 1.0),
        'state_conv': nrm(ks[4], (DEPTH, DEC_BATCH, CONV_K - 1, CONV_CH), 0.5),
        'meta_tokens': nrm(ks[5], (N_META, D_MODEL), 1.0),
        'norm_mix': 1.0 + nrm(ks[6], (DEPTH, D_MODEL), 0.02),
        'w_in': nrm(ks[7], (DEPTH, D_MODEL, IN_COLS), D_MODEL ** -0.5),
        'attn_sinks': nrm(ks[8], (DEPTH, N_HEADS), 0.5),
        'w_dw': nrm(ks[9], (DEPTH, CONV_K, CONV_CH), CONV_K ** -0.5),
        'b_dw': nrm(ks[10], (DEPTH, CONV_CH), 0.02),
        'conv_ln_g': 1.0 + nrm(ks[11], (DEPTH, CONV_CH), 0.02),
        'conv_ln_b': nrm(ks[12], (DEPTH, CONV_CH), 0.02),
        'w_out': nrm(ks[13], (DEPTH, ATTN_WIDTH + CONV_CH, D_MODEL), (ATTN_WIDTH + CONV_CH) ** -0.5),
        'norm_mlp': 1.0 + nrm(ks[14], (DEPTH, D_MODEL), 0.02),
        'w_up': nrm(ks[15], (DEPTH, D_MODEL, D_FF), D_MODEL ** -0.5),
        'w_down': nrm(ks[16], (DEPTH, D_FF, D_MODEL), D_FF ** -0.5),
        'norm_final': 1.0 + nrm(ks[17], (D_MODEL,), 0.02),
    }


def reference(x_prompt, x_sample, cache_k, cache_v, state_conv, meta_tokens, norm_mix, w_in, attn_sinks,
              w_dw, b_dw, conv_ln_g, conv_ln_b, w_out, norm_mlp, w_up, w_down, norm_final):
    n_prompt = x_prompt.shape[0]
    meta = jnp.broadcast_to(meta_tokens.astype(x_prompt.dtype)[None], (n_prompt, N_META, D_MODEL))
    hp = jnp.concatenate([meta, x_prompt], axis=1)
    hs = x_sample
    pos_p = jnp.arange(hp.shape[1], dtype=jnp.int32)
    pos_s = PAST_LEN + jnp.arange(hs.shape[1], dtype=jnp.int32)
    buf_len = cache_k.shape[2]
    kp, vp, cp, ksl, vsl, csl = [], [], [], [], [], []
    for l in range(DEPTH):
        q, k, v, u = mixer_inputs(hp, norm_mix[l], w_in[l])
        q = partial_rope(q, pos_p)
        k = partial_rope(k, pos_p)
        attn = swa_prompt(q, k, v, attn_sinks[l])
        u_hist = jnp.pad(u, ((0, 0), (CONV_K - 1, 0), (0, 0)))
        conv = conv_branch(u_hist, w_dw[l], b_dw[l], conv_ln_g[l], conv_ln_b[l])
        hp = hp + merge_groups(attn, conv, w_out[l])
        hp = hp + sq_relu_mlp(hp, norm_mlp[l], w_up[l], w_down[l])
        kp.append(k[:, -WINDOW:])
        vp.append(v[:, -WINDOW:])
        cp.append(u_hist[:, -(CONV_K - 1):])
        q, k, v, u = mixer_inputs(hs, norm_mix[l], w_in[l])
        q = partial_rope(q, pos_s)
        k = partial_rope(k, pos_s)
        kk = jnp.concatenate([cache_k[l].astype(k.dtype), k], axis=1)
        vv = jnp.concatenate([cache_v[l].astype(v.dtype), v], axis=1)
        attn = swa_sample(q, kk, vv, attn_sinks[l])
        u_hist = jnp.concatenate([state_conv[l].astype(u.dtype), u], axis=1)
        conv = conv_branch(u_hist, w_dw[l], b_dw[l], conv_ln_g[l], conv_ln_b[l])
        hs = hs + merge_groups(attn, conv, w_out[l])
        hs = hs + sq_relu_mlp(hs, norm_mlp[l], w_up[l], w_down[l])
        ksl.append(kk[:, -buf_len:])
        vsl.append(vv[:, -buf_len:])
        csl.append(u_hist[:, -(CONV_K - 1):])
    y_prompt = rms_norm(hp, norm_final)[:, N_META:]
    y_sample = rms_norm(hs, norm_final)
    return (y_prompt, y_sample, jnp.stack(kp), jnp.stack(vp), jnp.stack(cp),
            jnp.stack(ksl), jnp.stack(vsl), jnp.stack(csl))
```

```python
import numpy as np
from contextlib import ExitStack
import concourse.bass as bass
import concourse.mybir as mybir
from concourse.bass_utils import run_bass_kernel_spmd

F32 = mybir.dt.float32
BF16 = mybir.dt.bfloat16
AF = mybir.ActivationFunctionType
ALU = mybir.AluOpType

D = 2048
NCORES = 8
RECIP_MODE = "act"
MASK_MODE = "pe"
NT_DVE = 11


class _Op:
    __slots__ = ("eng", "fn", "deps", "sig", "semval", "dma", "dsem", "dval", "prewait")


class Sched:
    COMPUTE = ("pe", "act", "dve", "pool")
    QUEUES = ("sp", "pool", "act")
    KDMA = {"sp": 12, "pool": 8, "act": 4}
    ENGS = ("pe", "act", "dve", "pool", "sp")

    def __init__(self):
        self.ops = {e: [] for e in self.ENGS}
        self.last_w = {}
        self.rd_eng = {}
        self.rd_dma = {}
        self.bar_pending = {}

    def barrier(self):
        deps = []
        for e in self.ENGS:
            lst = self.ops[e]
            if lst:
                if e in self.COMPUTE:
                    for op in reversed(lst):
                        if not op.dma:
                            deps.append(op)
                            break
                if e in self.QUEUES:
                    k = 0
                    for op in reversed(lst):
                        if op.dma:
                            deps.append(op)
                            k += 1
                            if k >= self.KDMA[e]:
                                break
        for e in self.ENGS:
            if e == "pe":
                continue
            self.bar_pending[e] = list(self.bar_pending.get(e, [])) + deps

    def add(self, eng, fn, reads=(), writes=(), dma=False):
        op = _Op()
        op.eng = eng; op.fn = fn; op.dma = dma; op.sig = False
        op.semval = 0; op.dsem = None; op.dval = 0; op.prewait = None
        raw = []
        other = []
        for r in reads:
            w = self.last_w.get(r)
            if w is not None:
                raw.append(w)
        for w_ in writes:
            lw = self.last_w.get(w_)
            if lw is not None:
                other.append(lw)
            other.extend(self.rd_eng.get(w_, {}).values())
            other.extend(self.rd_dma.get(w_, ()))
        deps = []
        seen = set()
        for d in raw:
            if id(d) in seen:
                continue
            seen.add(id(d))
            if d.dma or dma or d.eng != eng or eng != "pe":
                deps.append(d)
        for d in other:
            if id(d) in seen:
                continue
            seen.add(id(d))
            if d.dma or dma or d.eng != eng:
                deps.append(d)
        if eng in self.bar_pending:
            for d in self.bar_pending.pop(eng):
                if id(d) not in seen and d is not op:
                    seen.add(id(d))
                    if d.dma or dma or d.eng != eng:
                        deps.append(d)
        op.deps = deps
        for d in deps:
            d.sig = True
        for r in reads:
            if dma:
                self.rd_dma.setdefault(r, []).append(op)
            else:
                self.rd_eng.setdefault(r, {})[eng] = op
        for w_ in writes:
            self.last_w[w_] = op
            self.rd_eng[w_] = {}
            self.rd_dma[w_] = []
        self.ops[eng].append(op)
        return op

    def emit(self, nc, es):
        csem = {e: es.enter_context(nc.semaphore("cs_" + e)) for e in self.COMPUTE}
        dsems = {q: [es.enter_context(nc.semaphore("ds_%s%d" % (q, i))) for i in range(self.KDMA[q])]
                 for q in self.QUEUES}
        for e in self.COMPUTE:
            cnt = 0
            for op in self.ops[e]:
                if op.sig and not op.dma:
                    cnt += 1
                    op.semval = cnt
        finals = {}
        for q in self.QUEUES:
            i = 0
            K = self.KDMA[q]
            for op in self.ops[q]:
                if op.dma:
                    op.dsem = dsems[q][i % K]
                    op.dval = 16 * (i // K + 1)
                    op.prewait = (op.dsem, 16 * (i // K)) if i >= K else None
                    finals[id(op.dsem)] = (op.dsem, op.dval)
                    i += 1

        def run(ename, eng, final=False):
            seen = {}
            for op in self.ops[ename]:
                need = {}
                for d in op.deps:
                    if d.dma:
                        s, v = d.dsem, d.dval
                    else:
                        s, v = csem[d.eng], d.semval
                    if need.get(id(s), (None, 0))[1] < v:
                        need[id(s)] = (s, v)
                if op.prewait is not None:
                    s, v = op.prewait
                    if need.get(id(s), (None, 0))[1] < v:
                        need[id(s)] = (s, v)
                for k, (s, v) in need.items():
                    if seen.get(k, 0) < v:
                        eng.wait_ge(s, v)
                        seen[k] = v
                ins = op.fn(eng)
                if op.dma:
                    ins.then_inc(op.dsem, 16)
                elif op.sig:
                    ins.then_inc(csem[ename], 1)
            if final:
                for k, (s, v) in finals.items():
                    if seen.get(k, 0) < v:
                        eng.wait_ge(s, v)

        with nc.Block() as block:
            @block.tensor
            def _(e):
                run("pe", e)

            @block.scalar
            def _(e):
                run("act", e)

            @block.vector
            def _(e):
                run("dve", e)

            @block.gpsimd
            def _(e):
                run("pool", e)

            @block.sync
            def _(e):
                run("sp", e, final=True)


def _a32(x):
    return (x + 31) // 32 * 32


def build_nc():
    nc = bass.Bass("TRN2", target_bir_lowering=False)
    S = Sched()

    def din(name, shape):
        return nc.dram_tensor(name, shape, F32, kind="ExternalInput").ap()

    def dout(name, shape):
        return nc.dram_tensor(name, shape, F32, kind="ExternalOutput").ap()

    xin = din("xin", [1280, D])
    ckT = din("ckT", [128, 16 * 2 * 128])
    cvK = din("cvK", [128, 16 * 4 * 2 * 64])
    ck_raw = din("ck_raw", [16, 128, 256])
    cv_raw = din("cv_raw", [16, 128, 256])
    stT = din("stT", [128, 8, 30, 16])
    st_raw = din("st_raw", [16, 30, 1024])
    masks_d = din("masks", [128, 5 * 128])
    ropecs_d = din("ropecs", [128, 10 * 2 * 16])
    convp_d = din("convp", [128, 8 * 34])
    gains_d = din("gains", [3, D])
    sinks_d = din("sinks", [16])
    w_in = din("w_in", [D, 3584])
    w_out = din("w_out", [D, D])
    w_up = din("w_up", [D, 8192])
    w_down = din("w_down", [8192, D])

    y_o = dout("y", [1152, D])
    nkp_o = dout("nkp", [128, 256])
    nvp_o = dout("nvp", [128, 256])
    ncp_o = dout("ncp", [32, 1024])
    nks_o = dout("nks", [16, 128, 256])
    nvs_o = dout("nvs", [16, 128, 256])
    ncs_o = dout("ncs", [16, 30, 1024])

    es = ExitStack()
    with es:
        cur = [_a32(nc.sbuf_base)]
        offs = {}
        top = nc.sbuf_top

        def alloc(name, shape, dtype, at=None):
            n = 1
            for s_ in shape[1:]:
                n *= s_
            nbytes = n * mybir.dt.size(dtype)
            off = cur[0] if at is None else at
            assert off % 32 == 0
            assert off + nbytes <= top, (name, off, nbytes, top)
            t = nc.alloc_sbuf_tensor_at(name, list(shape), dtype, offset=off)
            offs[name] = off
            if at is None:
                cur[0] = off + _a32(nbytes)
            return t

        xT = alloc("xT", [128, 16, 1280], BF16)
        AT = alloc("AT", [128, 16, 1152], BF16)
        W = [alloc("W%d" % i, [128, 16, 512], BF16) for i in range(2)]
        gbc = alloc("gbc", [128, D], F32)
        xnb = [alloc("xnb%d" % i, [128, D], BF16) for i in range(2)]
        ident = alloc("ident", [128, 128], BF16)
        identf = alloc("identf", [128, 128], F32)
        ones = alloc("ones", [128, 128], BF16)
        masks = alloc("masks", [128, 5, 128], BF16)
        ropecs = alloc("ropecs", [128, 10, 2, 16], F32)
        convp = alloc("convp", [128, 8, 34], F32)
        esink = alloc("esink", [128, 16], F32)
        stat = alloc("stat", [128, 64], F32)
        RB = cur[0]
        RSZ = top - RB
        assert RSZ >= 9 * D * 4 + 2 * 1536, RSZ

        def ralloc_reset():
            cur[0] = RB

        PB = [es.enter_context(nc.psum_tensor("pb%d" % i, [128, 512], F32)) for i in range(8)]

        def pbf(i):
            return PB[i][:].bitcast(BF16)[:, 0:512].rearrange("p (j t) -> p j t", j=4)

        def dma(q, out, in_, reads=(), writes=(), maxlast=None):
            if maxlast is None:
                return S.add(q, lambda e: e.dma_start(out=out, in_=in_), reads=reads, writes=writes, dma=True)
            return S.add(q, lambda e: e.dma_start(out=out, in_=in_, max_dma_last_dim=maxlast), reads=reads,
                         writes=writes, dma=True)

        def wv(wd, r0, c0):
            return wd[r0:r0 + 2048, c0:c0 + 512].rearrange("(kc p) n -> p kc n", p=128)

        wtiles = []
        for i in range(4):
            wtiles.append(("ab", i))
        for i in (2, 0, 1):
            wtiles.append(("qkv", i))
        for i in range(4):
            wtiles.append(("out", i))
        for s_ in range(4):
            for i in range(4):
                wtiles.append(("up", s_, i))
            for i in range(4):
                wtiles.append(("down", s_, i))
        wstate = {"next": 0}

        def wkeys(k):
            return [("W", k), ("Wb", k)]

        def wload():
            t = wstate["next"]
            if t >= len(wtiles):
                return
            wstate["next"] = t + 1
            k = t % 2
            wt = wtiles[t]
            both = wkeys(k)
            if wt[0] == "ab":
                i = wt[1]
                va = w_in[:, 1536 + 256 * i:1536 + 256 * (i + 1)].rearrange("(kc p) n -> p kc n", p=128)
                vb = w_in[:, 2560 + 256 * i:2560 + 256 * (i + 1)].rearrange("(kc p) n -> p kc n", p=128)
                dma("pool", W[k][:, :, 0:256], va, writes=[("W", k)])
                dma("pool", W[k][:, :, 256:512], vb, writes=[("Wb", k)])
            elif wt[0] == "qkv":
                dma("pool", W[k][:], wv(w_in, 0, 512 * wt[1]), writes=both)
            elif wt[0] == "out":
                dma("pool", W[k][:], wv(w_out, 0, 512 * wt[1]), writes=both)
            elif wt[0] == "up":
                dma("pool", W[k][:], wv(w_up, 0, wt[1] * 2048 + 512 * wt[2]), writes=both)
            else:
                dma("pool", W[k][:], wv(w_down, wt[1] * 2048, 512 * wt[2]), writes=both)

        wuse = {"t": 0}

        def wcur():
            return wuse["t"] % 2

        def wdone():
            wuse["t"] += 1
            wload()

        def norm_act(src, src_keys, blk, sc):
            k = blk % 3
            xb = xnb[k]
            ms = stat[:, sc:sc + 1]
            rs = stat[:, sc + 1:sc + 2]
            S.add("act", lambda e: e.activation(out=xb[:], in_=src, func=AF.Square,
                                                scale=float(1.0 / np.sqrt(D)), accum_out=ms),
                  reads=src_keys, writes=[("xnb", k), ("st", sc)])
            S.add("act", lambda e: e.activation(out=rs, in_=ms, func=AF.Sqrt, bias=1e-6),
                  reads=[("st", sc)], writes=[("st", sc + 1)])

        def norm_dve(src, src_keys, blk, sc):
            k = blk % 3
            xb = xnb[k]
            rs = stat[:, sc + 1:sc + 2]
            S.add("dve", lambda e: e.reciprocal(rs, rs), reads=[("st", sc + 1)], writes=[("st", sc + 1)])
            S.add("dve", lambda e: e.scalar_tensor_tensor(out=xb[:], in0=src, scalar=rs, in1=gbc[:],
                                                          op0=ALU.mult, op1=ALU.mult),
                  reads=list(src_keys) + [("st", sc + 1), "gbc"], writes=[("xnb", k)])

        def norm_chain(src, src_keys, blk, sc):
            norm_act(src, src_keys, blk, sc)
            norm_dve(src, src_keys, blk, sc)

        def norm_tr(blk, tb):
            k = blk % 3
            xb = xnb[k]
            for g in range(4):
                bank = tb[g % 2]

                def tr(e, g=g, bank=bank):
                    ins = None
                    for j in range(4):
                        c = g * 4 + j
                        ins = e.transpose(pbf(bank)[:, j, :], xb[:, c * 128:(c + 1) * 128], ident[:])
                    return ins
                S.add("pe", tr, reads=[("xnb", k), "ident"], writes=[("pb", bank)])
                dst = xT[:, g * 4:(g + 1) * 4, blk * 128:(blk + 1) * 128]
                if g % 2:
                    S.add("act", lambda e, dst=dst, bank=bank: e.activation(out=dst, in_=pbf(bank), func=AF.Copy),
                          reads=[("pb", bank)], writes=[("xT", blk)])
                else:
                    S.add("dve", lambda e, dst=dst, bank=bank: e.tensor_copy(dst, pbf(bank)),
                          reads=[("pb", bank)], writes=[("xT", blk)])

        S.add("pool", lambda e: e.memset(identf[:], 0.0), writes=["identf"])
        S.add("pool", lambda e: e.affine_select(identf[:], identf[:], pattern=[[-1, 128]], compare_op=ALU.not_equal,
                                                fill=1.0, base=0, channel_multiplier=1),
              reads=["identf"], writes=["identf"])
        S.add("pool", lambda e: e.memset(ones[:], 1.0), writes=["ones"])
        S.add("dve", lambda e: e.tensor_copy(ident[:], identf[:]), reads=["identf"], writes=["ident"])
        xs = [alloc("xs%d" % i, [128, D], F32, at=offs["AT"] + i * D * 4) for i in range(3)]
        xnb.append(alloc("xnb2a", [128, D], BF16, at=offs["AT"] + 3 * D * 4))
        for b0 in range(2):
            dma("sp", xs[b0][:], xin[b0 * 128:(b0 + 1) * 128, :], writes=[("xs", b0)])
        dma("sp", gbc[:], gains_d[0].partition_broadcast(128), writes=["gbc"])
        dma("sp", ropecs[:].rearrange("p a b c -> p (a b c)"), ropecs_d, writes=["ropecs"])
        dma("sp", convp[:].rearrange("p a b -> p (a b)"), convp_d, writes=["convp"])
        dma("sp", esink[:], sinks_d.partition_broadcast(128), writes=["esink"])
        S.add("act", lambda e: e.activation(out=esink[:], in_=esink[:], func=AF.Exp), reads=["esink"],
              writes=["esink"])
        dma("pool", masks[:].rearrange("p a b -> p (a b)"), masks_d, writes=["masks"])
        wload()
        wload()

        def p0_block(b):
            if b >= 2:
                dma("sp", xs[b % 3][:], xin[b * 128:(b + 1) * 128, :], writes=[("xs", b % 3)])
            if b >= 2:
                norm_tr(b - 2, (6, 7))
            norm_chain(xs[b % 3][:], [("xs", b % 3)], b, 2 * b)

        ralloc_reset()
        yT = alloc("yT", [128, 8, 1152], F32)
        uf = alloc("uf", [128, 8, 160], F32)
        QKB = cur[0]
        UT = [alloc("UT%d" % i, [128, 1664], BF16) for i in range(2)]
        sig = [alloc("sig%d" % i, [128, 416], F32) for i in range(2)]
        diag = [alloc("diag%d" % i, [128, 31, 128], BF16) for i in range(2)]
        acc = [alloc("acc%d" % i, [128, 1152], F32) for i in range(2)]
        TG = [(96, 480), (480, 864), (864, 1280)]
        bacnt = [0]

        def Hview(ut):
            return ut[:, 1056:1664].rearrange("p (j s) -> p j s", s=16)

        def ba_ops(c, tgi):
            t0, t1 = TG[tgi]
            n = t1 - t0
            k = bacnt[0] % 2
            bacnt[0] += 1
            cc = c % 2
            wk = wkeys(wcur())
            Wt = W[wcur()]
            bb, ba = PB[k], PB[2 + k]

            def mm(e, col, bank):
                ins = None
                for kc in range(16):
                    ins = e.matmul(bank[:, 0:n], Wt[:, kc, col:col + 128], xT[:, kc, t0:t1],
                                   start=(kc == 0), stop=(kc == 15))
                return ins
            xkeys = [("xT", b) for b in range(t0 // 128, (t1 + 127) // 128)]
            S.add("pe", lambda e: mm(e, 256 + cc * 128, bb), reads=wk + xkeys, writes=[("pb", k)])
            S.add("pe", lambda e: mm(e, cc * 128, ba), reads=wk + xkeys, writes=[("pb", 2 + k)])
            S.add("act", lambda e: e.activation(out=sig[k][:, 0:n], in_=bb[:, 0:n], func=AF.Sigmoid),
                  reads=[("pb", k)], writes=[("sig", k)])
            ut = UT[c % 2]
            ukey = ("UT", c % 2, tgi)
            if tgi < 2:
                S.add("dve", lambda e: e.tensor_tensor(out=ut[:, t0 - 96:t1 - 96], in0=ba[:, 0:n], in1=sig[k][:, 0:n],
                                                       op=ALU.mult),
                      reads=[("pb", 2 + k), ("sig", k)], writes=[ukey])
            else:
                S.add("dve", lambda e: e.tensor_tensor(out=ut[:, 768:1056], in0=ba[:, 0:288], in1=sig[k][:, 0:288],
                                                       op=ALU.mult),
                      reads=[("pb", 2 + k), ("sig", k)], writes=[ukey])
                S.add("dve", lambda e: e.tensor_tensor(
                    out=Hview(ut)[:, 30:38, :].rearrange("p t s -> p s t"),
                    in0=ba[:, 288:416].rearrange("p (s t) -> p s t", t=8),
                    in1=sig[k][:, 288:416].rearrange("p (s t) -> p s t", t=8), op=ALU.mult),
                    reads=[("pb", 2 + k), ("sig", k)], writes=[("UT", c % 2, "smp")])
                S.add("dve", lambda e: e.tensor_tensor(out=uf[:, c, :], in0=ba[:, 256:416], in1=sig[k][:, 256:416],
                                                       op=ALU.mult),
                      reads=[("pb", 2 + k), ("sig", k)], writes=[("uf", c)])

        cvcnt = [0]

        def nt_of(c):
            return NT_DVE if c < 7 else 6

        def conv_prep(c):
            ut = UT[c % 2]
            dma("pool", ut[:, 1056:1056 + 480], stT[:, c, :, :].rearrange("p j s -> p (j s)"),
                writes=[("UT", c % 2, "hist")])
            dg = diag[c % 2]
            ntd = nt_of(c)
            npe = 31 - ntd
            S.add("dve", lambda e: e.tensor_tensor(
                out=dg[:, ntd:31, :], in0=ident[:].unsqueeze(1).to_broadcast([128, npe, 128]),
                in1=convp[:, c, ntd:31].unsqueeze(2).to_broadcast([128, npe, 128]), op=ALU.mult),
                reads=["ident", "convp"], writes=[("diag", c % 2)])

        def conv_taps(c, j0, j1):
            ut = UT[c % 2]
            ac = acc[c % 2]
            for j in range(j0, j1):
                wj = convp[:, c, j:j + 1]
                for part in range(2):
                    if part == 0:
                        src = ut[:, 2 + j:2 + j + 1024]
                        dst = ac[:, 0:1024]
                        rk = [("UT", c % 2, 0), ("UT", c % 2, 1), ("UT", c % 2, 2)]
                        wkey = ("acc", c % 2, "p")
                    else:
                        src = ut[:, 1056 + j * 16:1056 + j * 16 + 128]
                        dst = ac[:, 1024:1152]
                        rk = [("UT", c % 2, "smp"), ("UT", c % 2, "hist")]
                        wkey = ("acc", c % 2, "s")
                    if j == 0:
                        S.add("dve", lambda e, src=src, dst=dst, wj=wj: e.tensor_scalar(dst, src, wj, None,
                                                                                       op0=ALU.mult),
                              reads=rk + ["convp"], writes=[wkey])
                    else:
                        S.add("dve", lambda e, src=src, dst=dst, wj=wj: e.scalar_tensor_tensor(
                            out=dst, in0=src, scalar=wj, in1=dst, op0=ALU.mult, op1=ALU.add),
                            reads=rk + ["convp", wkey], writes=[wkey])

        cvbank = {}

        def conv_mm(c, g):
            k = cvcnt[0] % 4
            cvcnt[0] += 1
            cvbank[(c, g)] = k
            bank = PB[4 + k]
            ut = UT[c % 2]
            dg = diag[c % 2]
            ntd = nt_of(c)
            if g < 2:
                tau0 = 32 + g * 512

                def mm(e):
                    ins = None
                    for j in range(ntd, 31):
                        ins = e.matmul(bank[:, 0:512], dg[:, j, :], ut[:, tau0 - 30 + j:tau0 - 30 + j + 512],
                                       start=(j == ntd), stop=(j == 30))
                    return ins
                rk = [("UT", c % 2, 0), ("UT", c % 2, 1), ("UT", c % 2, 2), ("diag", c % 2)]
                S.add("pe", mm, reads=rk, writes=[("pb", 4 + k)])
            else:
                def mm(e):
                    ins = None
                    for j in range(ntd, 31):
                        ins = e.matmul(bank[:, 0:128], dg[:, j, :],
                                       ut[:, 1056 + j * 16:1056 + j * 16 + 128], start=(j == ntd), stop=(j == 30))
                    return ins
                rk = [("UT", c % 2, "smp"), ("UT", c % 2, "hist"), ("diag", c % 2)]
                S.add("pe", mm, reads=rk, writes=[("pb", 4 + k)])

        def conv_comb(c, g):
            k = cvbank[(c, g)]
            bank = PB[4 + k]
            ac = acc[c % 2]
            bias = convp[:, c, 31:32]
            if g < 2:
                S.add("dve", lambda e: e.scalar_tensor_tensor(
                    out=yT[:, c, g * 512:(g + 1) * 512], in0=bank[:, 0:512], scalar=bias,
                    in1=ac[:, g * 512:(g + 1) * 512], op0=ALU.add, op1=ALU.add),
                    reads=[("pb", 4 + k), "convp", ("acc", c % 2, "p")], writes=[("yT", c, g)])
            else:
                S.add("dve", lambda e: e.scalar_tensor_tensor(
                    out=yT[:, c, 1024:1152].rearrange("p (s t) -> p s t", t=8),
                    in0=bank[:, 0:128].rearrange("p (t s) -> p s t", s=16), scalar=bias,
                    in1=ac[:, 1024:1152].rearrange("p (t s) -> p s t", s=16), op0=ALU.add, op1=ALU.add),
                    reads=[("pb", 4 + k), "convp", ("acc", c % 2, "s")], writes=[("yT", c, 2)])

        for b in range(0, 6):
            p0_block(b)
        conv_prep(0)
        ba_ops(0, 0)
        for b in range(6, 9):
            p0_block(b)
        ba_ops(0, 1)
        p0_block(9)
        norm_tr(8, (6, 7))
        norm_tr(9, (6, 7))
        ba_ops(0, 2)
        for c in range(8):
            if c + 1 < 8:
                if (c + 1) % 2 == 0:
                    wdone()
                conv_prep(c + 1)
            ntd_ = nt_of(c)
            tb_ = [0, ntd_ // 3, 2 * ntd_ // 3, ntd_]
            for g in range(3):
                if c + 1 < 8:
                    ba_ops(c + 1, g)
                conv_taps(c, tb_[g], tb_[g + 1])
                conv_mm(c, g)
            for g in range(3):
                conv_comb(c, g)
        wdone()

        S.barrier()

        ao = offs["AT"]
        mean = alloc("mean", [128, 384], F32, at=ao)
        rstd = alloc("rstd", [128, 384], F32, at=ao + 1536)
        msq = alloc("msq", [128, 384], F32, at=ao + 3072)
        yb = alloc("yb", [128, 8, 384], BF16, at=ao + 4608)
        ysq = alloc("ysq", [128, 8, 384], BF16, at=ao + 10752)
        t1 = [alloc("t1_%d" % i, [128, 384], F32, at=offs["xnb0"] + i * 1536) for i in range(2)]
        uo = alloc("uo", [128, 1024], F32, at=offs["xnb1"])
        for half in range(2):
            def trp(e, half=half):
                ins = None
                for j in range(4):
                    c = half * 4 + j
                    ins = e.transpose(PB[6 + half][0:32, j * 128:(j + 1) * 128], uf[:, c, 0:32], identf[:])
                return ins
            S.add("pe", trp, reads=[("uf", c) for c in range(8)] + ["identf", "ufall"], writes=[("pb", 6 + half)])
            S.add("dve", lambda e, half=half: e.tensor_copy(uo[0:32, half * 512:(half + 1) * 512],
                                                            PB[6 + half][0:32, :]),
                  reads=[("pb", 6 + half)], writes=[("uo", half)])
        dma("sp", ncp_o, uo[0:32, :], reads=[("uo", 0), ("uo", 1)])
        for half in range(2):
            def trs(e, half=half):
                ins = None
                for j in range(4):
                    c = half * 4 + j
                    ins = e.transpose(PB[4 + half][:, j * 128:(j + 1) * 128], uf[:, c, 32:160], identf[:])
                return ins
            S.add("pe", trs, reads=[("uf", c) for c in range(8)] + ["identf", "ufall"], writes=[("pb", 4 + half)])
            S.add("act", lambda e, half=half: e.activation(out=uo[:, half * 512:(half + 1) * 512], in_=PB[4 + half][:],
                                                           func=AF.Copy),
                  reads=[("pb", 4 + half)], writes=[("uo", half)])
        for s_ in range(16):
            dma("sp", ncs_o[s_, 22:30, :], uo[s_ * 8:(s_ + 1) * 8, :], reads=[("uo", 0), ("uo", 1)])

        ln_steps = []
        ykeys = [("yT", c, g) for c in range(8) for g in range(3)]

        def ln_Ac(tg, c):
            c0, c1 = tg * 384, (tg + 1) * 384
            S.add("act", lambda e: e.activation(out=yb[:, c, :], in_=yT[:, c, c0:c1], func=AF.Copy),
                  reads=ykeys, writes=[("yb", c)])
            S.add("act", lambda e: e.activation(out=ysq[:, c, :], in_=yT[:, c, c0:c1], func=AF.Square),
                  reads=ykeys, writes=[("ysq", c)])

        def ln_A(tg):
            for c in range(8):
                ln_Ac(tg, c)

        def ln_B(tg):
            i1, i2 = 4, 5
            b1, b2 = PB[i1], PB[i2]

            def mms(e, bank, src):
                ins = None
                for c in range(8):
                    ins = e.matmul(bank[:, 0:384], ones[:], src[:, c, :], start=(c == 0), stop=(c == 7))
                return ins
            S.add("pe", lambda e: mms(e, b1, yb), reads=[("yb", c) for c in range(8)] + ["ones"],
                  writes=[("pb", i1)])
            S.add("pe", lambda e: mms(e, b2, ysq), reads=[("ysq", c) for c in range(8)] + ["ones"],
                  writes=[("pb", i2)])
            S.add("act", lambda e: e.activation(out=mean[:], in_=b1[:, 0:384], func=AF.Identity, scale=1.0 / 1024),
                  reads=[("pb", i1)], writes=["mean"])
            S.add("dve", lambda e: e.tensor_tensor(out=msq[:], in0=mean[:], in1=mean[:], op=ALU.mult),
                  reads=["mean"], writes=["msq"])
            S.add("dve", lambda e: e.scalar_tensor_tensor(out=rstd[:], in0=b2[:, 0:384], scalar=1.0 / 1024,
                                                          in1=msq[:], op0=ALU.mult, op1=ALU.subtract),
                  reads=[("pb", i2), "msq"], writes=["rstd"])
            S.add("act", lambda e: e.activation(out=rstd[:], in_=rstd[:], func=AF.Sqrt, bias=1e-5),
                  reads=["rstd"], writes=["rstd"])
            S.add("dve", lambda e: e.reciprocal(rstd[:], rstd[:]), reads=["rstd"], writes=["rstd"])

        def ln_C(tg, c):
            c0, c1 = tg * 384, (tg + 1) * 384
            tt = t1[c % 2]
            S.add("dve", lambda e: e.tensor_tensor(out=tt[:], in0=yT[:, c, c0:c1], in1=mean[:], op=ALU.subtract),
                  reads=ykeys + ["mean"], writes=[("t1", c % 2)])
            S.add("dve", lambda e: e.tensor_tensor(out=tt[:], in0=tt[:], in1=rstd[:], op=ALU.mult),
                  reads=[("t1", c % 2), "rstd"], writes=[("t1", c % 2)])
            S.add("act", lambda e: e.activation(
                out=AT[:, 8 + c, c0:c1], in_=tt[:], func=AF.Silu, scale=convp[:, c, 32:33],
                bias=convp[:, c, 33:34]),
                reads=[("t1", c % 2), "convp"], writes=[("AT", 8 + c, 3 * tg + i) for i in range(3)])

        ln_Ac(0, 0)
        ln_Ac(0, 1)
        for c in range(2, 8):
            ln_steps.append(lambda c=c: ln_Ac(0, c))
        for tg in range(3):
            ln_steps.append(lambda tg=tg: ln_B(tg))
            for c in range(8):
                ln_steps.append(lambda tg=tg, c=c: ln_C(tg, c))
                if tg < 2:
                    ln_steps.append(lambda tg=tg, c=c: ln_Ac(tg + 1, c))

        cur[0] = QKB
        QT = alloc("QT", [128, 2, 9, 512], BF16)
        KT = alloc("KT", [128, 2, 1280], BF16)
        Vd = alloc("Vd", [128, 10, 4, 128], BF16)
        kvf = [alloc("kvf%d" % i, [128, 512], F32) for i in range(2)]
        qb = [alloc("qb%d" % i, [128, 512], BF16, at=offs["uf"] + i * 1024) for i in range(3)]
        rt = [alloc("rt%d" % i, [128, 4, 64], F32, at=offs["uf"] + 3072 + i * 1024) for i in range(2)]
        qcnt = [0]

        def rope_ops(k, psrc, nh, b, dsts, dkeys, perm=False, kb=0):
            r = rt[k][:].rearrange("p a b -> p (a b)").rearrange("p (h a d) -> p h a d", a=2, d=16)[:, 0:nh]
            for a_ in range(2):
                tab = ropecs[:, b, a_, :].unsqueeze(1).to_broadcast([128, nh, 16])
                S.add("dve", lambda e, a_=a_, tab=tab: e.tensor_tensor(out=r[:, :, a_, :], in0=psrc[:, :, 0:16],
                                                                      in1=tab, op=ALU.mult),
                      reads=[("pb", kb), "ropecs", ("actrd", k)], writes=[("rt", k), "ufall"])
            if perm:
                rv = r.rearrange("p (u i) a d -> p u i a d", u=2)
                ra = [rv[:, :, :, 0, 0:8], rv[:, :, :, 0, 8:16]]
                rb = [rv[:, :, :, 1, 8:16], rv[:, :, :, 1, 0:8]]
            else:
                ra = [r[:, :, 0, 0:8], r[:, :, 0, 8:16]]
                rb = [r[:, :, 1, 8:16], r[:, :, 1, 0:8]]
            for dst, dk_ in zip(dsts, dkeys):
                if perm:
                    dv = dst.rearrange("p (i u d) -> p u i d", u=2, d=64)
                    do = [dv[:, :, :, 0:8], dv[:, :, :, 8:16]]
                else:
                    dv = dst.rearrange("p (h d) -> p h d", d=64)
                    do = [dv[:, :, 0:8], dv[:, :, 8:16]]
                for j in range(2):
                    S.add("dve", lambda e, j=j, do=do: e.tensor_tensor(out=do[j], in0=ra[j], in1=rb[j], op=ALU.add),
                          reads=[("rt", k)], writes=[dk_])

        def qkv_s1(ti, b):
            i_ = qcnt[0]
            qcnt[0] += 1
            k = i_ % 2
            k3 = i_ % 3
            Wt = W[wcur()]
            wk = wkeys(wcur())
            kb = (0, 1, 6, 7)[i_ % 4]
            bank = PB[kb]
            stg = kvf[k]

            def mm(e):
                ins = None
                for kc in range(16):
                    ins = e.matmul(bank[:], xT[:, kc, b * 128:(b + 1) * 128], Wt[:, kc, :],
                                   start=(kc == 0), stop=(kc == 15))
                return ins
            S.add("pe", mm, reads=wk + [("xT", b)], writes=[("pb", kb)])
            return (ti, b, k, k3, bank, stg, kb)

        def qkv_s1b(ctx):
            ti, b, k, k3, bank, stg, kb = ctx
            if ti < 2:
                S.add("act", lambda e: e.activation(
                    out=qb[k3][:].rearrange("p (i u d) -> p u i d", u=2, d=64),
                    in_=bank[:].rearrange("p (u i d) -> p u i d", u=2, d=64), func=AF.Copy),
                    reads=[("pb", kb)], writes=[("qb", k3), "ufall", ("actrd", k)])
                rope_ops(k, bank[:].rearrange("p (h d) -> p h d", d=64), 8, b, [qb[k3][:]], [("qb", k3)], perm=True, kb=kb)
            else:
                outp = b in (8, 9)
                S.add("act", lambda e: e.activation(out=qb[k3][:, 0:256], in_=bank[:, 0:256], func=AF.Copy),
                      reads=[("pb", kb)], writes=[("qb", k3), "ufall", ("actrd", k)])
                for u in range(2):
                    S.add("act", lambda e, u=u: e.activation(
                        out=Vd[:, b, :, :].rearrange("p h (u d) -> p h u d", u=2)[:, :, u, :],
                        in_=bank[:, 256:512].rearrange("p (h d) -> p h d", d=64), func=AF.Copy),
                        reads=[("pb", kb)], writes=[("Vd", b), ("actrd", k)])
                dsts = [qb[k3][:, 0:256]]
                dkeys = [("qb", k3)]
                if outp:
                    S.add("act", lambda e: e.activation(out=stg[:], in_=bank[:], func=AF.Copy),
                          reads=[("pb", kb)], writes=[("kvf", k), ("actrd", k)])
                    dsts.append(stg[:, 0:256])
                    dkeys.append(("kvf", k))
                rope_ops(k, bank[:, 0:256].rearrange("p (h d) -> p h d", d=64), 4, b, dsts, dkeys, kb=kb)
                if b == 8:
                    dma("sp", nkp_o, stg[:, 0:256], reads=[("kvf", k)])
                    dma("sp", nvp_o, stg[:, 256:512], reads=[("kvf", k)])
                if b == 9:
                    for s_ in range(16):
                        dma("sp", nks_o[s_, 120:128, :], stg[s_ * 8:(s_ + 1) * 8, 0:256], reads=[("kvf", k)])
                        dma("sp", nvs_o[s_, 120:128, :], stg[s_ * 8:(s_ + 1) * 8, 256:512], reads=[("kvf", k)])

        def qkv_s2(ctx):
            ti, b, k, k3, bank, stg, kb = ctx
            tb = 2 + k
            n = 4 if ti < 2 else 2

            def tr(e):
                ins = None
                for i in range(n):
                    ins = e.transpose(pbf(tb)[:, i, :], qb[k3][:, i * 128:(i + 1) * 128], ident[:])
                return ins
            S.add("pe", tr, reads=[("qb", k3), "ident"], writes=[("pb", tb)])
            if ti < 2:
                if b < 9:
                    S.add("act", lambda e: e.activation(
                        out=QT[:, ti, b - 1, :].rearrange("p (i t) -> p i t", i=4), in_=pbf(tb), func=AF.Copy),
                        reads=[("pb", tb)], writes=[("QT", ti, b)])
                else:
                    S.add("act", lambda e: e.activation(
                        out=QT[:, ti, 8, :].rearrange("p (s i t) -> p i s t", i=4, t=8),
                        in_=pbf(tb).rearrange("p i (s t) -> p i s t", t=8), func=AF.Copy),
                        reads=[("pb", tb)], writes=[("QT", ti, b)])
            else:
                S.add("act", lambda e: e.activation(out=KT[:, :, b * 128:(b + 1) * 128], in_=pbf(tb)[:, 0:2, :],
                                                    func=AF.Copy),
                      reads=[("pb", tb)], writes=[("KT", b)])

        pend = []
        qitems = [(ti, b) for ti in (2, 0, 1) for b in range(0 if ti == 2 else 1, 10)]
        nq = len(qitems)
        nl = len(ln_steps)
        li = 0
        for qi, (ti, b) in enumerate(qitems):
            ctx = qkv_s1(ti, b)
            if len(pend) >= 2:
                qkv_s2(pend.pop(0))
            tgt = (qi + 1) * nl // nq
            while li < tgt:
                ln_steps[li]()
                li += 1
            qkv_s1b(ctx)
            pend.append(ctx)
            if qi + 1 == nq or qitems[qi + 1][0] != ti:
                wdone()
        while li < nl:
            ln_steps[li]()
            li += 1
        MB = [alloc("MB%d" % m, [128, 512], BF16, at=(offs["xnb0"] + m * 1024) if m < 4 else offs["xnb1"])
              for m in range(5)]
        for m in range(5):
            if m < 3:
                ov = MB[m][:].rearrange("p (i t) -> p i t", i=4)
                iv = masks[:, m, :].unsqueeze(1).to_broadcast([128, 4, 128])
            elif m == 3:
                ov = MB[m][:].rearrange("p (s i t) -> p s i t", i=4, t=8)
                iv = masks[:, 3, :].rearrange("p (s t) -> p s t", t=8).unsqueeze(2).to_broadcast([128, 16, 4, 8])
            else:
                ov = MB[m][:].rearrange("p (a t) -> p a t", t=8)
                iv = masks[:, 4, 0:8].unsqueeze(1).to_broadcast([128, 64, 8])
            S.add("dve", lambda e, ov=ov, iv=iv: e.tensor_scalar(ov, iv, 30000.0, -30000.0, op0=ALU.mult,
                                                                 op1=ALU.add),
                  reads=["masks"], writes=[("MB", m), ("xnb", 0), ("xnb", 1), ("t1", 0), ("t1", 1), ("uo", 0), ("uo", 1)])

        while pend:
            qkv_s2(pend.pop(0))

        S.barrier()
        ralloc_reset()
        KcT = alloc("KcT", [128, 16, 2, 128], BF16)
        Vcd = alloc("Vcd", [128, 16, 4, 128], BF16)
        PT = [[alloc("PT%d_%d" % (i, j), [128, 512], BF16) for j in range(2)] for i in range(2)]
        den = [alloc("den%d" % i, [128, 512], F32) for i in range(2)]
        rscr = alloc("rscr", [128, 512], F32)
        S.add("pool", lambda e: e.memset(rscr[:], -1.0), writes=["rscr"])
        dma("pool", KcT[:].rearrange("p a b c -> p a (b c)"), ckT.rearrange("p (a x) -> p a x", a=16),
            writes=["KcT"], maxlast=2048)
        dma("pool", Vcd[:].rearrange("p s h d -> p s (h d)"), cvK.rearrange("p (s x) -> p s x", s=16),
            writes=[("Vcd", 0), ("Vcd", 1)], maxlast=2048)

        acnt = [0]

        def attn_A(qbk, kvh):
            k = acnt[0] % 2
            acnt[0] += 1
            jp, h0 = kvh // 2, (kvh % 2) * 64
            tq = (qbk - 1) * 128
            smp = (qbk == 9)
            qrhs = QT[h0:h0 + 64, jp, qbk - 1, :]
            qkeys = [("QT", jp, qbk)]
            sP, sC, bO, bD = PB[k], PB[2 + k], PB[4 + k], PB[6 + k]
            ptP, ptC = PT[k][0], PT[k][1]
            mC = 3 if smp else 0
            mP = 4 if smp else (2 if qbk == 1 else 1)

            pe_mask = (MASK_MODE == "pe")
            pe_mask_c = MASK_MODE in ("pe", "hybrid")

            def qk_c(e):
                if pe_mask_c:
                    e.matmul(sC[:], ident[:], MB[mC][:], start=True, stop=False)
                return e.matmul(sC[:], KT[h0:h0 + 64, jp, qbk * 128:(qbk + 1) * 128], qrhs, start=not pe_mask_c,
                                stop=True)
            S.add("pe", qk_c, reads=qkeys + [("KT", qbk), ("MB", mC), "ident"], writes=[("pb", 2 + k)])
            if not smp:
                def qk_p(e):
                    if pe_mask:
                        e.matmul(sP[:], ident[:], MB[mP][:], start=True, stop=False)
                    return e.matmul(sP[:], KT[h0:h0 + 64, jp, (qbk - 1) * 128:qbk * 128], qrhs, start=not pe_mask,
                                    stop=True)
                S.add("pe", qk_p, reads=qkeys + [("KT", qbk - 1), ("MB", mP), "ident"], writes=[("pb", k)])
            else:
                def qk_p(e):
                    ins = None
                    if pe_mask:
                        ins = e.matmul(sP[:], ident[:], MB[mP][:], start=True, stop=False)
                    for s_ in range(16):
                        ins = e.matmul(sP[:, s_ * 32:(s_ + 1) * 32], KcT[h0:h0 + 64, s_, jp, :],
                                       QT[h0:h0 + 64, jp, 8, s_ * 32:(s_ + 1) * 32], start=not pe_mask,
                                       stop=(s_ == 15) or not pe_mask)
                    return ins
                S.add("pe", qk_p, reads=qkeys + ["KcT", ("MB", mP), "ident"], writes=[("pb", k)])
            S.add("act", lambda e: e.activation(out=ptC[:], in_=sC[:], func=AF.Exp, scale=0.125),
                  reads=[("pb", 2 + k)], writes=[("PT", k, 1)])
            S.add("act", lambda e: e.activation(out=ptP[:], in_=sP[:], func=AF.Exp, scale=0.125),
                  reads=[("pb", k)], writes=[("PT", k, 0)])
            if not pe_mask:
                if not smp:
                    vC = ptC[:].rearrange("p (i t) -> p i t", i=4)
                    vP = ptP[:].rearrange("p (i t) -> p i t", i=4)
                    mCv = masks[:, mC, :].unsqueeze(1).to_broadcast([128, 4, 128])
                    mPv = masks[:, mP, :].unsqueeze(1).to_broadcast([128, 4, 128])
                else:
                    vC = ptC[:].rearrange("p (s i t) -> p s i t", i=4, t=8)
                    vP = ptP[:].rearrange("p (a t) -> p a t", t=8)
                    mCv = masks[:, 3, :].rearrange("p (s t) -> p s t", t=8).unsqueeze(2).to_broadcast(
                        [128, 16, 4, 8])
                    mPv = masks[:, 4, 0:8].unsqueeze(1).to_broadcast([128, 64, 8])
                if not pe_mask_c:
                    S.add("pool", lambda e: e.tensor_tensor(out=vC, in0=vC, in1=mCv, op=ALU.mult),
                          reads=[("PT", k, 1), "masks"], writes=[("PT", k, 1)])
                S.add("pool", lambda e: e.tensor_tensor(out=vP, in0=vP, in1=mPv, op=ALU.mult),
                      reads=[("PT", k, 0), "masks"], writes=[("PT", k, 0)])
            return (qbk, kvh, k, tq, bO, bD, ptP, ptC)

        def attn_B(ctx):
            qbk, kvh, k, tq, bO, bD, ptP, ptC = ctx
            smp = (qbk == 9)
            if not smp:
                def pv(e):
                    e.matmul(bO[:], Vd[:, qbk, kvh, :], ptC[:], start=True, stop=False)
                    return e.matmul(bO[:], Vd[:, qbk - 1, kvh, :], ptP[:], start=False, stop=True)
                S.add("pe", pv, reads=[("PT", k, 0), ("PT", k, 1), ("Vd", qbk), ("Vd", qbk - 1)],
                      writes=[("pb", 4 + k)])
            else:
                def pv(e):
                    ins = e.matmul(bO[:], Vd[:, 9, kvh, :], ptC[:], start=True, stop=False)
                    for s_ in range(16):
                        ins = e.matmul(bO[:, s_ * 32:(s_ + 1) * 32], Vcd[:, s_, kvh, :],
                                       ptP[:, s_ * 32:(s_ + 1) * 32], start=False, stop=(s_ == 15))
                    return ins
                S.add("pe", pv, reads=[("PT", k, 0), ("PT", k, 1), ("Vd", 9), ("Vcd", 0), ("Vcd", 1)],
                      writes=[("pb", 4 + k)])

            def dn(e):
                e.matmul(bD[:], ones[:], ptC[:], start=True, stop=False)
                return e.matmul(bD[:], ones[:], ptP[:], start=False, stop=True)
            S.add("pe", dn, reads=[("PT", k, 0), ("PT", k, 1), "ones"], writes=[("pb", 6 + k)])
            dk = den[k]
            dkeys = [("den", k)]
            if not smp:
                bDv = bD[:].rearrange("p (i t) -> p i t", i=4)
                bOv = bO[:].rearrange("p (i t) -> p i t", i=4)
                dv = dk[:, 0:256].rearrange("p (i t) -> p i t", i=2)
                adds = [(dv[hs * 64:(hs + 1) * 64], bDv[hs * 64:(hs + 1) * 64, hs:4:2, :],
                         esink[hs * 64:(hs + 1) * 64, 4 * kvh + hs:4 * kvh + 4:2].unsqueeze(2).to_broadcast(
                             [64, 2, 128])) for hs in range(2)]
                outs = [AT[hs * 64:(hs + 1) * 64, 2 * kvh:2 * kvh + 2, tq:tq + 128] for hs in range(2)]
                in0s = [bOv[hs * 64:(hs + 1) * 64, hs:4:2, :] for hs in range(2)]
                in1s = [dv[hs * 64:(hs + 1) * 64] for hs in range(2)]
            else:
                bD4 = bD[:].rearrange("p (s i t) -> p s i t", i=4, t=8)
                dv4 = dk[:, 0:256].rearrange("p (s i t) -> p s i t", i=2, t=8)
                adds = [(dv4[hs * 64:(hs + 1) * 64], bD4[hs * 64:(hs + 1) * 64, :, hs:4:2, :],
                         esink[hs * 64:(hs + 1) * 64, 4 * kvh + hs:4 * kvh + 4:2].unsqueeze(1).unsqueeze(
                             3).to_broadcast([64, 16, 2, 8])) for hs in range(2)]
                bOp = bO[:].rearrange("p (s i t) -> p i s t", i=4, t=8)
                dvp = dk[:, 0:256].rearrange("p (s i t) -> p i s t", i=2, t=8)
                outs = [AT[hs * 64:(hs + 1) * 64, 2 * kvh:2 * kvh + 2, tq:tq + 128].rearrange(
                    "p c (s t) -> p c s t", t=8) for hs in range(2)]
                in0s = [bOp[hs * 64:(hs + 1) * 64, hs:4:2, :, :] for hs in range(2)]
                in1s = [dvp[hs * 64:(hs + 1) * 64] for hs in range(2)]
            for hs in range(2):
                S.add("dve", lambda e, hs=hs: e.tensor_tensor(out=adds[hs][0], in0=adds[hs][1], in1=adds[hs][2],
                                                              op=ALU.add),
                      reads=[("pb", 6 + k), "esink"], writes=dkeys)
            S.add("act", lambda e: e.activation(out=dk[:, 0:256], in_=dk[:, 0:256], func=AF.Ln), reads=dkeys,
                  writes=dkeys)
            S.add("act", lambda e: e.activation(out=dk[:, 0:256], in_=dk[:, 0:256], func=AF.Exp, scale=-1.0),
                  reads=dkeys, writes=dkeys)
            for hs in range(2):
                S.add("dve", lambda e, hs=hs: e.tensor_tensor(out=outs[hs], in0=in0s[hs], in1=in1s[hs],
                                                              op=ALU.mult),
                      reads=[("pb", 4 + k)] + dkeys,
                      writes=[("AT", 2 * kvh, qbk - 1), ("AT", 2 * kvh + 1, qbk - 1)])

        items = [(qbk, kvh) for qbk in range(1, 10) for kvh in range(4)]
        prev_ctx = None
        for it in items:
            ctx = attn_A(*it)
            if prev_ctx is not None:
                attn_B(prev_ctx)
            prev_ctx = ctx
        attn_B(prev_ctx)

        S.barrier()
        ralloc_reset()
        h = alloc("h", [128, 9, D], F32)
        rtmp = [alloc("rtmp%d" % i, [128, 384], F32, at=offs["xnb%d" % i]) for i in range(2)]
        xnb[2] = alloc("xnb2b", [128, D], BF16)
        dma("sp", gbc[:], gains_d[1].partition_broadcast(128), writes=["gbc"])
        for dg in range(4):
            for b in range(1, 10):
                dma("sp", h[:, b - 1, dg * 512:(dg + 1) * 512], xin[b * 128:(b + 1) * 128, dg * 512:(dg + 1) * 512],
                    writes=[("h", b, dg)])
        ocnt = [0]
        for dg in range(4):
            Wt = W[wcur()]
            wk = wkeys(wcur())
            for b in range(1, 10):
                k = ocnt[0] % 2
                ocnt[0] += 1
                bank = PB[k]

                def mm(e, b=b, bank=bank, Wt=Wt):
                    ins = None
                    for c in range(16):
                        ins = e.matmul(bank[:], AT[:, c, (b - 1) * 128:b * 128], Wt[:, c, :],
                                       start=(c == 0), stop=(c == 15))
                    return ins
                S.add("pe", mm, reads=wk + [("AT", c, b - 1) for c in range(16)], writes=[("pb", k)])
                hv = h[:, b - 1, dg * 512:(dg + 1) * 512]
                S.add("dve", lambda e, hv=hv, bank=bank: e.tensor_tensor(out=hv, in0=hv, in1=bank[:], op=ALU.add),
                      reads=[("pb", k), ("h", b, dg)], writes=[("h", b, dg)])
                if dg == 3:
                    hsrc = lambda bb: (h[:, bb - 1, :], [("h", bb, i) for i in range(4)], bb, 20 + 2 * bb)
                    if b >= 4:
                        norm_tr(b - 3, (6, 7))
                    norm_act(*hsrc(b))
                    if b >= 2:
                        norm_dve(*hsrc(b - 1))
            if dg == 3:
                norm_dve(*hsrc(9))
                norm_tr(7, (6, 7))
                norm_tr(8, (6, 7))
                norm_tr(9, (6, 7))
            wdone()

        def fin_act(b):
            sc = 40 + 2 * b
            ms = stat[:, sc:sc + 1]
            rs = stat[:, sc + 1:sc + 2]
            hb = h[:, b - 1, :]
            hk = [("h", b, i) for i in range(4)]
            kx = b % 2
            S.add("act", lambda e: e.activation(out=xnb[kx][:], in_=hb, func=AF.Square,
                                                scale=float(1.0 / np.sqrt(D)), accum_out=ms),
                  reads=hk, writes=[("xnb", kx), ("st", sc)])
            S.add("act", lambda e: e.activation(out=rs, in_=ms, func=AF.Sqrt, bias=1e-6),
                  reads=[("st", sc)], writes=[("st", sc + 1)])

        def fin_dve(b):
            sc = 40 + 2 * b
            rs = stat[:, sc + 1:sc + 2]
            hb = h[:, b - 1, :]
            hk = [("h", b, i) for i in range(4)]
            S.add("dve", lambda e: e.reciprocal(rs, rs), reads=[("st", sc + 1)], writes=[("st", sc + 1)])
            S.add("dve", lambda e: e.scalar_tensor_tensor(out=hb, in0=hb, scalar=rs, in1=gbc[:], op0=ALU.mult,
                                                          op1=ALU.mult),
                  reads=hk + [("st", sc + 1), "gbc"], writes=hk)
            dma("sp", y_o[(b - 1) * 128:b * 128, :], hb, reads=hk)

        ucnt = [0]
        dcnt = [0]
        dma("sp", nks_o[:, 0:120, :], ck_raw[:, 8:128, :])
        dma("sp", nvs_o[:, 0:120, :], cv_raw[:, 8:128, :])
        dma("sp", ncs_o[:, 0:22, :], st_raw[:, 8:30, :])
        for s_ in range(4):
            for ft in range(4):
                Wt = W[wcur()]
                wk = wkeys(wcur())
                for fc in range(4):
                    fcg = ft * 4 + fc
                    k = ucnt[0] % 2
                    ucnt[0] += 1
                    banks = [PB[3 * k + i] for i in range(3)]

                    def mm(e, Wt=Wt, fc=fc, banks=banks):
                        ins = None
                        for kc in range(16):
                            for tg in range(3):
                                ins = e.matmul(banks[tg][:, 0:384], Wt[:, kc, fc * 128:(fc + 1) * 128],
                                               xT[:, kc, 128 + tg * 384:128 + (tg + 1) * 384],
                                               start=(kc == 0), stop=(kc == 15))
                        return ins
                    S.add("pe", mm, reads=wk + [("xT", b) for b in range(1, 10)],
                          writes=[("pb", 3 * k + i) for i in range(3)])
                    for tg in range(3):
                        r = rtmp[tg % 2]
                        S.add("act", lambda e, r=r, bank=banks[tg]: e.activation(out=r[:], in_=bank[:, 0:384],
                                                                                func=AF.Relu),
                              reads=[("pb", 3 * k + tg)], writes=[("rtmp", tg % 2), ("xnb", tg % 2)])
                        S.add("dve", lambda e, r=r, fcg=fcg, tg=tg: e.tensor_tensor(
                            out=AT[:, fcg, tg * 384:(tg + 1) * 384], in0=r[:], in1=r[:], op=ALU.mult),
                            reads=[("rtmp", tg % 2), ("xnb", tg % 2)],
                            writes=[("AT", fcg, 3 * tg + i) for i in range(3)])
                wdone()
            for dg in range(4):
                Wt = W[wcur()]
                wk = wkeys(wcur())
                last = (s_ == 3)
                if last and dg == 0:
                    dma("sp", gbc[:], gains_d[2].partition_broadcast(128), writes=["gbc"])
                for b in range(1, 10):
                    k = dcnt[0] % 2
                    dcnt[0] += 1
                    bank = PB[6 + k]

                    def mm(e, b=b, bank=bank, Wt=Wt):
                        ins = None
                        for c in range(16):
                            ins = e.matmul(bank[:], AT[:, c, (b - 1) * 128:b * 128], Wt[:, c, :],
                                           start=(c == 0), stop=(c == 15))
                        return ins
                    S.add("pe", mm, reads=wk + [("AT", c, b - 1) for c in range(16)], writes=[("pb", 6 + k)])
                    hv = h[:, b - 1, dg * 512:(dg + 1) * 512]
                    S.add("dve", lambda e, hv=hv, bank=bank: e.tensor_tensor(out=hv, in0=hv, in1=bank[:], op=ALU.add),
                          reads=[("pb", 6 + k), ("h", b, dg)], writes=[("h", b, dg)])
                    if last and dg == 3:
                        fin_act(b)
                        if b >= 2:
                            fin_dve(b - 1)
                if last and dg == 3:
                    fin_dve(9)
                wdone()

        S.emit(nc, es)
    return nc


_NC_CACHE = {}


def _consts(hf):
    f32 = np.float32
    pos = np.zeros((10, 128), dtype=f32)
    i = np.arange(128)
    pos[0] = (i - 112) if hf == 0 else (912 + i)
    for b in range(1, 9):
        pos[b] = 16 + hf * 1024 + (b - 1) * 128 + i
    pos[9] = 8192 + (i % 8)
    inv = np.power(f32(500000.0), -np.arange(8, dtype=f32) * f32(2.0) / f32(16)).astype(f32)
    ang = (pos[:, :, None].astype(f32) * inv[None, None, :]).astype(f32)
    co, si = np.cos(ang).astype(f32), np.sin(ang).astype(f32)
    cs = np.stack([np.concatenate([co, co], axis=-1), np.concatenate([si, -si], axis=-1)], axis=2)
    ropecs = np.ascontiguousarray(cs.transpose(1, 0, 2, 3)).reshape(128, 320).astype(f32)
    j = np.arange(128)[:, None]
    q = np.arange(128)[None, :]
    m = np.zeros((128, 5, 128), dtype=f32)
    m[:, 0] = (j <= q)
    m[:, 1] = (j > q)
    m[:, 2] = (j > q) & ((j >= 112) if hf == 0 else True)
    m[:, 3] = ((j // 8) == (q // 8)) & ((j % 8) <= (q % 8))
    m[:, 4] = (j > (q % 8))
    return ropecs, m.reshape(128, 640)


def kernel(x_prompt, x_sample, cache_k, cache_v, state_conv, meta_tokens, norm_mix, w_in, attn_sinks,
           w_dw, b_dw, conv_ln_g, conv_ln_b, w_out, norm_mlp, w_up, w_down, norm_final):
    f32 = np.float32
    x_prompt = np.asarray(x_prompt, f32); x_sample = np.asarray(x_sample, f32)
    cache_k = np.asarray(cache_k, f32); cache_v = np.asarray(cache_v, f32)
    state_conv = np.asarray(state_conv, f32)
    if "nc" not in _NC_CACHE:
        _NC_CACHE["nc"] = build_nc()
    nc = _NC_CACHE["nc"]

    w_in_ = np.ascontiguousarray(np.asarray(w_in, f32)[0])
    w_out_ = np.ascontiguousarray(np.asarray(w_out, f32)[0])
    w_up_ = np.ascontiguousarray(np.asarray(w_up, f32)[0])
    w_down_ = np.ascontiguousarray(np.asarray(w_down, f32)[0])
    gains = np.stack([np.asarray(norm_mix, f32)[0], np.asarray(norm_mlp, f32)[0],
                      np.asarray(norm_final, f32)]).astype(f32)
    sinks = np.ascontiguousarray(np.asarray(attn_sinks, f32)[0])
    cp = np.concatenate([np.asarray(w_dw, f32)[0], np.asarray(b_dw, f32), np.asarray(conv_ln_g, f32),
                         np.asarray(conv_ln_b, f32)], axis=0)
    convp = np.ascontiguousarray(cp.reshape(34, 8, 128).transpose(2, 1, 0)).reshape(128, 8 * 34)
    meta = np.asarray(meta_tokens, f32)
    cst = [_consts(0), _consts(1)]

    in_maps = []
    for c in range(NCORES):
        n, hf = c // 2, c % 2
        xin = np.zeros((1280, D), dtype=f32)
        if hf == 0:
            xin[112:128] = meta
        else:
            xin[0:128] = x_prompt[n, 896:1024]
        xin[128:1152] = x_prompt[n, hf * 1024:(hf + 1) * 1024]
        xin[1152:1280] = x_sample[16 * c:16 * (c + 1)].reshape(128, D)
        ck = cache_k[0, 16 * c:16 * (c + 1)]
        cv = cache_v[0, 16 * c:16 * (c + 1)]
        ckT = np.ascontiguousarray(ck.reshape(16, 128, 2, 2, 64).transpose(3, 4, 0, 2, 1)).reshape(128, 16 * 2 * 128)
        cvK = cv.reshape(16, 128, 4, 1, 64).transpose(1, 0, 2, 3, 4)
        cvK = np.ascontiguousarray(np.broadcast_to(cvK, (128, 16, 4, 2, 64))).reshape(128, 16 * 4 * 2 * 64)
        st = state_conv[0, 16 * c:16 * (c + 1)]
        stT = np.ascontiguousarray(st.reshape(16, 30, 8, 128).transpose(3, 2, 1, 0))
        in_maps.append({
            "xin": xin, "ckT": ckT, "cvK": cvK,
            "ck_raw": np.ascontiguousarray(ck.reshape(16, 128, 256)),
            "cv_raw": np.ascontiguousarray(cv.reshape(16, 128, 256)),
            "stT": stT, "st_raw": np.ascontiguousarray(st),
            "masks": cst[hf][1], "ropecs": cst[hf][0], "convp": convp, "gains": gains, "sinks": sinks,
            "w_in": w_in_, "w_out": w_out_, "w_up": w_up_, "w_down": w_down_,
        })
    res = run_bass_kernel_spmd(nc, in_maps, core_ids=list(range(NCORES)))
    R = res.results

    y_prompt = np.zeros((4, 2048, D), f32)
    y_sample = np.zeros((128, 8, D), f32)
    nkp = np.zeros((1, 4, 128, 4, 64), f32)
    nvp = np.zeros((1, 4, 128, 4, 64), f32)
    ncp = np.zeros((1, 4, 30, 1024), f32)
    nks = np.zeros((1, 128, 128, 4, 64), f32)
    nvs = np.zeros((1, 128, 128, 4, 64), f32)
    ncs = np.zeros((1, 128, 30, 1024), f32)
    for c in range(NCORES):
        n, hf = c // 2, c % 2
        r = R[c]
        y_prompt[n, hf * 1024:(hf + 1) * 1024] = r["y"][0:1024]
        y_sample[16 * c:16 * (c + 1)] = r["y"][1024:1152].reshape(16, 8, D)
        if hf == 1:
            nkp[0, n] = r["nkp"].reshape(128, 4, 64)
            nvp[0, n] = r["nvp"].reshape(128, 4, 64)
            ncp[0, n] = r["ncp"][2:32]
        nks[0, 16 * c:16 * (c + 1)] = r["nks"].reshape(16, 128, 4, 64)
        nvs[0, 16 * c:16 * (c + 1)] = r["nvs"].reshape(16, 128, 4, 64)
        ncs[0, 16 * c:16 * (c + 1)] = r["ncs"]
    return (y_prompt, y_sample, nkp, nvp, ncp, nks, nvs, ncs)
```

```python
import numpy as np
from contextlib import ExitStack
import concourse.bass as bass
import concourse.mybir as mybir
from concourse.bass_utils import run_bass_kernel_spmd

F32 = mybir.dt.float32
BF16 = mybir.dt.bfloat16
AF = mybir.ActivationFunctionType
ALU = mybir.AluOpType

D = 2048
NCORES = 8
RECIP_MODE = "act"
MASK_MODE = "pe"
NT_DVE = 12


class _Op:
    __slots__ = ("eng", "fn", "deps", "sig", "semval", "dma", "dsem", "dval", "prewait")


class Sched:
    COMPUTE = ("pe", "act", "dve", "pool")
    QUEUES = ("sp", "pool", "act")
    KDMA = {"sp": 12, "pool": 8, "act": 4}
    ENGS = ("pe", "act", "dve", "pool", "sp")

    def __init__(self):
        self.ops = {e: [] for e in self.ENGS}
        self.last_w = {}
        self.rd_eng = {}
        self.rd_dma = {}
        self.bar_pending = {}

    def barrier(self):
        deps = []
        for e in self.ENGS:
            lst = self.ops[e]
            if lst:
                if e in self.COMPUTE:
                    for op in reversed(lst):
                        if not op.dma:
                            deps.append(op)
                            break
                if e in self.QUEUES:
                    k = 0
                    for op in reversed(lst):
                        if op.dma:
                            deps.append(op)
                            k += 1
                            if k >= self.KDMA[e]:
                                break
        for e in self.ENGS:
            if e == "pe":
                continue
            self.bar_pending[e] = list(self.bar_pending.get(e, [])) + deps

    def add(self, eng, fn, reads=(), writes=(), dma=False):
        op = _Op()
        op.eng = eng; op.fn = fn; op.dma = dma; op.sig = False
        op.semval = 0; op.dsem = None; op.dval = 0; op.prewait = None
        raw = []
        other = []
        for r in reads:
            w = self.last_w.get(r)
            if w is not None:
                raw.append(w)
        for w_ in writes:
            lw = self.last_w.get(w_)
            if lw is not None:
                other.append(lw)
            other.extend(self.rd_eng.get(w_, {}).values())
            other.extend(self.rd_dma.get(w_, ()))
        deps = []
        seen = set()
        for d in raw:
            if id(d) in seen:
                continue
            seen.add(id(d))
            if d.dma or dma or d.eng != eng or eng != "pe":
                deps.append(d)
        for d in other:
            if id(d) in seen:
                continue
            seen.add(id(d))
            if d.dma or dma or d.eng != eng:
                deps.append(d)
        if eng in self.bar_pending:
            for d in self.bar_pending.pop(eng):
                if id(d) not in seen and d is not op:
                    seen.add(id(d))
                    if d.dma or dma or d.eng != eng:
                        deps.append(d)
        op.deps = deps
        for d in deps:
            d.sig = True
        for r in reads:
            if dma:
                self.rd_dma.setdefault(r, []).append(op)
            else:
                self.rd_eng.setdefault(r, {})[eng] = op
        for w_ in writes:
            self.last_w[w_] = op
            self.rd_eng[w_] = {}
            self.rd_dma[w_] = []
        self.ops[eng].append(op)
        return op

    def emit(self, nc, es):
        csem = {e: es.enter_context(nc.semaphore("cs_" + e)) for e in self.COMPUTE}
        dsems = {q: [es.enter_context(nc.semaphore("ds_%s%d" % (q, i))) for i in range(self.KDMA[q])]
                 for q in self.QUEUES}
        for e in self.COMPUTE:
            cnt = 0
            for op in self.ops[e]:
                if op.sig and not op.dma:
                    cnt += 1
                    op.semval = cnt
        finals = {}
        for q in self.QUEUES:
            i = 0
            K = self.KDMA[q]
            for op in self.ops[q]:
                if op.dma:
                    op.dsem = dsems[q][i % K]
                    op.dval = 16 * (i // K + 1)
                    op.prewait = (op.dsem, 16 * (i // K)) if i >= K else None
                    finals[id(op.dsem)] = (op.dsem, op.dval)
                    i += 1

        def run(ename, eng, final=False):
            seen = {}
            for op in self.ops[ename]:
                need = {}
                for d in op.deps:
                    if d.dma:
                        s, v = d.dsem, d.dval
                    else:
                        s, v = csem[d.eng], d.semval
                    if need.get(id(s), (None, 0))[1] < v:
                        need[id(s)] = (s, v)
                if op.prewait is not None:
                    s, v = op.prewait
                    if need.get(id(s), (None, 0))[1] < v:
                        need[id(s)] = (s, v)
                for k, (s, v) in need.items():
                    if seen.get(k, 0) < v:
                        eng.wait_ge(s, v)
                        seen[k] = v
                ins = op.fn(eng)
                if op.dma:
                    ins.then_inc(op.dsem, 16)
                elif op.sig:
                    ins.then_inc(csem[ename], 1)
            if final:
                for k, (s, v) in finals.items():
                    if seen.get(k, 0) < v:
                        eng.wait_ge(s, v)

        with nc.Block() as block:
            @block.tensor
            def _(e):
                run("pe", e)

            @block.scalar
            def _(e):
                run("act", e)

            @block.vector
            def _(e):
                run("dve", e)

            @block.gpsimd
            def _(e):
                run("pool", e)

            @block.sync
            def _(e):
                run("sp", e, final=True)


def _a32(x):
    return (x + 31) // 32 * 32


def build_nc():
    nc = bass.Bass("TRN2", target_bir_lowering=False)
    S = Sched()

    def din(name, shape):
        return nc.dram_tensor(name, shape, F32, kind="ExternalInput").ap()

    def dout(name, shape):
        return nc.dram_tensor(name, shape, F32, kind="ExternalOutput").ap()

    xin = din("xin", [1280, D])
    ckT = din("ckT", [128, 16 * 2 * 128])
    cvK = din("cvK", [128, 16 * 4 * 2 * 64])
    ck_raw = din("ck_raw", [16, 128, 256])
    cv_raw = din("cv_raw", [16, 128, 256])
    stT = din("stT", [128, 8, 30, 16])
    st_raw = din("st_raw", [16, 30, 1024])
    masks_d = din("masks", [128, 5 * 128])
    ropecs_d = din("ropecs", [128, 10 * 2 * 16])
    convp_d = din("convp", [128, 8 * 34])
    gains_d = din("gains", [3, D])
    sinks_d = din("sinks", [16])
    w_in = din("w_in", [D, 3584])
    w_out = din("w_out", [D, D])
    w_up = din("w_up", [D, 8192])
    w_down = din("w_down", [8192, D])

    y_o = dout("y", [1152, D])
    nkp_o = dout("nkp", [128, 256])
    nvp_o = dout("nvp", [128, 256])
    ncp_o = dout("ncp", [32, 1024])
    nks_o = dout("nks", [16, 128, 256])
    nvs_o = dout("nvs", [16, 128, 256])
    ncs_o = dout("ncs", [16, 30, 1024])

    es = ExitStack()
    with es:
        cur = [_a32(nc.sbuf_base)]
        offs = {}
        top = nc.sbuf_top

        def alloc(name, shape, dtype, at=None):
            n = 1
            for s_ in shape[1:]:
                n *= s_
            nbytes = n * mybir.dt.size(dtype)
            off = cur[0] if at is None else at
            assert off % 32 == 0
            assert off + nbytes <= top, (name, off, nbytes, top)
            t = nc.alloc_sbuf_tensor_at(name, list(shape), dtype, offset=off)
            offs[name] = off
            if at is None:
                cur[0] = off + _a32(nbytes)
            return t

        xT = alloc("xT", [128, 16, 1280], BF16)
        AT = alloc("AT", [128, 16, 1152], BF16)
        W = [alloc("W%d" % i, [128, 16, 512], BF16) for i in range(2)]
        gbc = alloc("gbc", [128, D], F32)
        xnb = [alloc("xnb%d" % i, [128, D], BF16) for i in range(2)]
        ident = alloc("ident", [128, 128], BF16)
        identf = alloc("identf", [128, 128], F32)
        ones = alloc("ones", [128, 128], BF16)
        masks = alloc("masks", [128, 5, 128], BF16)
        ropecs = alloc("ropecs", [128, 10, 2, 16], F32)
        convp = alloc("convp", [128, 8, 34], F32)
        esink = alloc("esink", [128, 16], F32)
        stat = alloc("stat", [128, 64], F32)
        RB = cur[0]
        RSZ = top - RB
        assert RSZ >= 9 * D * 4 + 2 * 1536, RSZ

        def ralloc_reset():
            cur[0] = RB

        PB = [es.enter_context(nc.psum_tensor("pb%d" % i, [128, 512], F32)) for i in range(8)]

        def pbf(i):
            return PB[i][:].bitcast(BF16)[:, 0:512].rearrange("p (j t) -> p j t", j=4)

        def dma(q, out, in_, reads=(), writes=(), maxlast=None):
            if maxlast is None:
                return S.add(q, lambda e: e.dma_start(out=out, in_=in_), reads=reads, writes=writes, dma=True)
            return S.add(q, lambda e: e.dma_start(out=out, in_=in_, max_dma_last_dim=maxlast), reads=reads,
                         writes=writes, dma=True)

        def wv(wd, r0, c0):
            return wd[r0:r0 + 2048, c0:c0 + 512].rearrange("(kc p) n -> p kc n", p=128)

        wtiles = []
        for i in range(4):
            wtiles.append(("ab", i))
        for i in (2, 0, 1):
            wtiles.append(("qkv", i))
        for i in range(4):
            wtiles.append(("out", i))
        for s_ in range(4):
            for i in range(4):
                wtiles.append(("up", s_, i))
            for i in range(4):
                wtiles.append(("down", s_, i))
        wstate = {"next": 0}

        def wkeys(k):
            return [("W", k), ("Wb", k)]

        def wload():
            t = wstate["next"]
            if t >= len(wtiles):
                return
            wstate["next"] = t + 1
            k = t % 2
            wt = wtiles[t]
            both = wkeys(k)
            if wt[0] == "ab":
                i = wt[1]
                va = w_in[:, 1536 + 256 * i:1536 + 256 * (i + 1)].rearrange("(kc p) n -> p kc n", p=128)
                vb = w_in[:, 2560 + 256 * i:2560 + 256 * (i + 1)].rearrange("(kc p) n -> p kc n", p=128)
                dma("pool", W[k][:, :, 0:256], va, writes=[("W", k)])
                dma("pool", W[k][:, :, 256:512], vb, writes=[("Wb", k)])
            elif wt[0] == "qkv":
                dma("pool", W[k][:], wv(w_in, 0, 512 * wt[1]), writes=both)
            elif wt[0] == "out":
                dma("pool", W[k][:], wv(w_out, 0, 512 * wt[1]), writes=both)
            elif wt[0] == "up":
                dma("pool", W[k][:], wv(w_up, 0, wt[1] * 2048 + 512 * wt[2]), writes=both)
            else:
                dma("pool", W[k][:], wv(w_down, wt[1] * 2048, 512 * wt[2]), writes=both)

        wuse = {"t": 0}

        def wcur():
            return wuse["t"] % 2

        def wdone():
            wuse["t"] += 1
            wload()

        def norm_act(src, src_keys, blk, sc):
            k = blk % 3
            xb = xnb[k]
            ms = stat[:, sc:sc + 1]
            rs = stat[:, sc + 1:sc + 2]
            S.add("act", lambda e: e.activation(out=xb[:], in_=src, func=AF.Square,
                                                scale=float(1.0 / np.sqrt(D)), accum_out=ms),
                  reads=src_keys, writes=[("xnb", k), ("st", sc)])
            S.add("act", lambda e: e.activation(out=rs, in_=ms, func=AF.Sqrt, bias=1e-6),
                  reads=[("st", sc)], writes=[("st", sc + 1)])

        def norm_dve(src, src_keys, blk, sc):
            k = blk % 3
            xb = xnb[k]
            rs = stat[:, sc + 1:sc + 2]
            S.add("dve", lambda e: e.reciprocal(rs, rs), reads=[("st", sc + 1)], writes=[("st", sc + 1)])
            S.add("dve", lambda e: e.scalar_tensor_tensor(out=xb[:], in0=src, scalar=rs, in1=gbc[:],
                                                          op0=ALU.mult, op1=ALU.mult),
                  reads=list(src_keys) + [("st", sc + 1), "gbc"], writes=[("xnb", k)])

        def norm_chain(src, src_keys, blk, sc):
            norm_act(src, src_keys, blk, sc)
            norm_dve(src, src_keys, blk, sc)

        def norm_tr(blk, tb):
            k = blk % 3
            xb = xnb[k]
            for g in range(4):
                bank = tb[g % 2]

                def tr(e, g=g, bank=bank):
                    ins = None
                    for j in range(4):
                        c = g * 4 + j
                        ins = e.transpose(pbf(bank)[:, j, :], xb[:, c * 128:(c + 1) * 128], ident[:])
                    return ins
                S.add("pe", tr, reads=[("xnb", k), "ident"], writes=[("pb", bank)])
                dst = xT[:, g * 4:(g + 1) * 4, blk * 128:(blk + 1) * 128]
                if g % 2:
                    S.add("act", lambda e, dst=dst, bank=bank: e.activation(out=dst, in_=pbf(bank), func=AF.Copy),
                          reads=[("pb", bank)], writes=[("xT", blk)])
                else:
                    S.add("dve", lambda e, dst=dst, bank=bank: e.tensor_copy(dst, pbf(bank)),
                          reads=[("pb", bank)], writes=[("xT", blk)])

        S.add("pool", lambda e: e.memset(identf[:], 0.0), writes=["identf"])
        S.add("pool", lambda e: e.affine_select(identf[:], identf[:], pattern=[[-1, 128]], compare_op=ALU.not_equal,
                                                fill=1.0, base=0, channel_multiplier=1),
              reads=["identf"], writes=["identf"])
        S.add("pool", lambda e: e.memset(ones[:], 1.0), writes=["ones"])
        S.add("dve", lambda e: e.tensor_copy(ident[:], identf[:]), reads=["identf"], writes=["ident"])
        dma("sp", gbc[:], gains_d[0].partition_broadcast(128), writes=["gbc"])
        dma("sp", ropecs[:].rearrange("p a b c -> p (a b c)"), ropecs_d, writes=["ropecs"])
        dma("sp", convp[:].rearrange("p a b -> p (a b)"), convp_d, writes=["convp"])
        dma("sp", esink[:], sinks_d.partition_broadcast(128), writes=["esink"])
        S.add("act", lambda e: e.activation(out=esink[:], in_=esink[:], func=AF.Exp), reads=["esink"],
              writes=["esink"])
        dma("pool", masks[:].rearrange("p a b -> p (a b)"), masks_d, writes=["masks"])
        wload()
        wload()

        xs = [alloc("xs%d" % i, [128, D], F32, at=offs["AT"] + i * D * 4) for i in range(3)]
        xnb.append(alloc("xnb2a", [128, D], BF16, at=offs["AT"] + 3 * D * 4))
        def p0_block(b):
            dma("sp", xs[b % 3][:], xin[b * 128:(b + 1) * 128, :], writes=[("xs", b % 3)])
            if b >= 2:
                norm_tr(b - 2, (6, 7))
            norm_chain(xs[b % 3][:], [("xs", b % 3)], b, 2 * b)

        ralloc_reset()
        yT = alloc("yT", [128, 8, 1152], F32)
        uf = alloc("uf", [128, 8, 160], F32)
        QKB = cur[0]
        UT = [alloc("UT%d" % i, [128, 1664], BF16) for i in range(2)]
        sig = [alloc("sig%d" % i, [128, 416], F32) for i in range(2)]
        diag = [alloc("diag%d" % i, [128, 31, 128], BF16) for i in range(2)]
        acc = [alloc("acc%d" % i, [128, 1152], F32) for i in range(2)]
        TG = [(96, 480), (480, 864), (864, 1280)]
        bacnt = [0]

        def Hview(ut):
            return ut[:, 1056:1664].rearrange("p (j s) -> p j s", s=16)

        def ba_ops(c, tgi):
            t0, t1 = TG[tgi]
            n = t1 - t0
            k = bacnt[0] % 2
            bacnt[0] += 1
            cc = c % 2
            wk = wkeys(wcur())
            Wt = W[wcur()]
            bb, ba = PB[k], PB[2 + k]

            def mm(e, col, bank):
                ins = None
                for kc in range(16):
                    ins = e.matmul(bank[:, 0:n], Wt[:, kc, col:col + 128], xT[:, kc, t0:t1],
                                   start=(kc == 0), stop=(kc == 15))
                return ins
            xkeys = [("xT", b) for b in range(t0 // 128, (t1 + 127) // 128)]
            S.add("pe", lambda e: mm(e, 256 + cc * 128, bb), reads=wk + xkeys, writes=[("pb", k)])
            S.add("pe", lambda e: mm(e, cc * 128, ba), reads=wk + xkeys, writes=[("pb", 2 + k)])
            S.add("act", lambda e: e.activation(out=sig[k][:, 0:n], in_=bb[:, 0:n], func=AF.Sigmoid),
                  reads=[("pb", k)], writes=[("sig", k)])
            ut = UT[c % 2]
            ukey = ("UT", c % 2, tgi)
            if tgi < 2:
                S.add("dve", lambda e: e.tensor_tensor(out=ut[:, t0 - 96:t1 - 96], in0=ba[:, 0:n], in1=sig[k][:, 0:n],
                                                       op=ALU.mult),
                      reads=[("pb", 2 + k), ("sig", k)], writes=[ukey])
            else:
                S.add("dve", lambda e: e.tensor_tensor(out=ut[:, 768:1056], in0=ba[:, 0:288], in1=sig[k][:, 0:288],
                                                       op=ALU.mult),
                      reads=[("pb", 2 + k), ("sig", k)], writes=[ukey])
                S.add("dve", lambda e: e.tensor_tensor(
                    out=Hview(ut)[:, 30:38, :].rearrange("p t s -> p s t"),
                    in0=ba[:, 288:416].rearrange("p (s t) -> p s t", t=8),
                    in1=sig[k][:, 288:416].rearrange("p (s t) -> p s t", t=8), op=ALU.mult),
                    reads=[("pb", 2 + k), ("sig", k)], writes=[("UT", c % 2, "smp")])
                S.add("dve", lambda e: e.tensor_tensor(out=uf[:, c, :], in0=ba[:, 256:416], in1=sig[k][:, 256:416],
                                                       op=ALU.mult),
                      reads=[("pb", 2 + k), ("sig", k)], writes=[("uf", c)])

        cvcnt = [0]

        def nt_of(c):
            return NT_DVE if c < 7 else 6

        def conv_prep(c):
            ut = UT[c % 2]
            dma("pool", ut[:, 1056:1056 + 480], stT[:, c, :, :].rearrange("p j s -> p (j s)"),
                writes=[("UT", c % 2, "hist")])
            dg = diag[c % 2]
            ntd = nt_of(c)
            npe = 31 - ntd
            S.add("dve", lambda e: e.tensor_tensor(
                out=dg[:, ntd:31, :], in0=ident[:].unsqueeze(1).to_broadcast([128, npe, 128]),
                in1=convp[:, c, ntd:31].unsqueeze(2).to_broadcast([128, npe, 128]), op=ALU.mult),
                reads=["ident", "convp"], writes=[("diag", c % 2)])

        def conv_taps(c, j0, j1):
            ut = UT[c % 2]
            ac = acc[c % 2]
            for j in range(j0, j1):
                wj = convp[:, c, j:j + 1]
                for part in range(2):
                    if part == 0:
                        src = ut[:, 2 + j:2 + j + 1024]
                        dst = ac[:, 0:1024]
                        rk = [("UT", c % 2, 0), ("UT", c % 2, 1), ("UT", c % 2, 2)]
                        wkey = ("acc", c % 2, "p")
                    else:
                        src = ut[:, 1056 + j * 16:1056 + j * 16 + 128]
                        dst = ac[:, 1024:1152]
                        rk = [("UT", c % 2, "smp"), ("UT", c % 2, "hist")]
                        wkey = ("acc", c % 2, "s")
                    if j == 0:
                        S.add("dve", lambda e, src=src, dst=dst, wj=wj: e.tensor_scalar(dst, src, wj, None,
                                                                                       op0=ALU.mult),
                              reads=rk + ["convp"], writes=[wkey])
                    else:
                        S.add("dve", lambda e, src=src, dst=dst, wj=wj: e.scalar_tensor_tensor(
                            out=dst, in0=src, scalar=wj, in1=dst, op0=ALU.mult, op1=ALU.add),
                            reads=rk + ["convp", wkey], writes=[wkey])

        cvbank = {}

        def conv_mm(c, g):
            k = cvcnt[0] % 4
            cvcnt[0] += 1
            cvbank[(c, g)] = k
            bank = PB[4 + k]
            ut = UT[c % 2]
            dg = diag[c % 2]
            ntd = nt_of(c)
            if g < 2:
                tau0 = 32 + g * 512

                def mm(e):
                    ins = None
                    for j in range(ntd, 31):
                        ins = e.matmul(bank[:, 0:512], dg[:, j, :], ut[:, tau0 - 30 + j:tau0 - 30 + j + 512],
                                       start=(j == ntd), stop=(j == 30))
                    return ins
                rk = [("UT", c % 2, 0), ("UT", c % 2, 1), ("UT", c % 2, 2), ("diag", c % 2)]
                S.add("pe", mm, reads=rk, writes=[("pb", 4 + k)])
            else:
                def mm(e):
                    ins = None
                    for j in range(ntd, 31):
                        ins = e.matmul(bank[:, 0:128], dg[:, j, :],
                                       ut[:, 1056 + j * 16:1056 + j * 16 + 128], start=(j == ntd), stop=(j == 30))
                    return ins
                rk = [("UT", c % 2, "smp"), ("UT", c % 2, "hist"), ("diag", c % 2)]
                S.add("pe", mm, reads=rk, writes=[("pb", 4 + k)])

        def conv_comb(c, g):
            k = cvbank[(c, g)]
            bank = PB[4 + k]
            ac = acc[c % 2]
            bias = convp[:, c, 31:32]
            if g < 2:
                S.add("dve", lambda e: e.scalar_tensor_tensor(
                    out=yT[:, c, g * 512:(g + 1) * 512], in0=bank[:, 0:512], scalar=bias,
                    in1=ac[:, g * 512:(g + 1) * 512], op0=ALU.add, op1=ALU.add),
                    reads=[("pb", 4 + k), "convp", ("acc", c % 2, "p")], writes=[("yT", c, g)])
            else:
                S.add("dve", lambda e: e.scalar_tensor_tensor(
                    out=yT[:, c, 1024:1152].rearrange("p (s t) -> p s t", t=8),
                    in0=bank[:, 0:128].rearrange("p (t s) -> p s t", s=16), scalar=bias,
                    in1=ac[:, 1024:1152].rearrange("p (t s) -> p s t", s=16), op0=ALU.add, op1=ALU.add),
                    reads=[("pb", 4 + k), "convp", ("acc", c % 2, "s")], writes=[("yT", c, 2)])

        for b in range(0, 6):
            p0_block(b)
        conv_prep(0)
        ba_ops(0, 0)
        for b in range(6, 9):
            p0_block(b)
        ba_ops(0, 1)
        p0_block(9)
        norm_tr(8, (6, 7))
        norm_tr(9, (6, 7))
        ba_ops(0, 2)
        for c in range(8):
            if c + 1 < 8:
                if (c + 1) % 2 == 0:
                    wdone()
                conv_prep(c + 1)
            ntd_ = nt_of(c)
            tb_ = [0, ntd_ // 3, 2 * ntd_ // 3, ntd_]
            for g in range(3):
                if c + 1 < 8:
                    ba_ops(c + 1, g)
                conv_taps(c, tb_[g], tb_[g + 1])
                conv_mm(c, g)
            for g in range(3):
                conv_comb(c, g)
        wdone()

        S.barrier()

        ao = offs["AT"]
        mean = alloc("mean", [128, 384], F32, at=ao)
        rstd = alloc("rstd", [128, 384], F32, at=ao + 1536)
        msq = alloc("msq", [128, 384], F32, at=ao + 3072)
        yb = alloc("yb", [128, 8, 384], BF16, at=ao + 4608)
        ysq = alloc("ysq", [128, 8, 384], BF16, at=ao + 10752)
        t1 = [alloc("t1_%d" % i, [128, 384], F32, at=offs["xnb0"] + i * 1536) for i in range(2)]
        uo = alloc("uo", [128, 1024], F32, at=offs["xnb1"])
        for half in range(2):
            def trp(e, half=half):
                ins = None
                for j in range(4):
                    c = half * 4 + j
                    ins = e.transpose(PB[6 + half][0:32, j * 128:(j + 1) * 128], uf[:, c, 0:32], identf[:])
                return ins
            S.add("pe", trp, reads=[("uf", c) for c in range(8)] + ["identf", "ufall"], writes=[("pb", 6 + half)])
            S.add("dve", lambda e, half=half: e.tensor_copy(uo[0:32, half * 512:(half + 1) * 512],
                                                            PB[6 + half][0:32, :]),
                  reads=[("pb", 6 + half)], writes=[("uo", half)])
        dma("sp", ncp_o, uo[0:32, :], reads=[("uo", 0), ("uo", 1)])
        for half in range(2):
            def trs(e, half=half):
                ins = None
                for j in range(4):
                    c = half * 4 + j
                    ins = e.transpose(PB[4 + half][:, j * 128:(j + 1) * 128], uf[:, c, 32:160], identf[:])
                return ins
            S.add("pe", trs, reads=[("uf", c) for c in range(8)] + ["identf", "ufall"], writes=[("pb", 4 + half)])
            S.add("act", lambda e, half=half: e.activation(out=uo[:, half * 512:(half + 1) * 512], in_=PB[4 + half][:],
                                                           func=AF.Copy),
                  reads=[("pb", 4 + half)], writes=[("uo", half)])
        for s_ in range(16):
            dma("sp", ncs_o[s_, 22:30, :], uo[s_ * 8:(s_ + 1) * 8, :], reads=[("uo", 0), ("uo", 1)])

        ln_steps = []
        ykeys = [("yT", c, g) for c in range(8) for g in range(3)]

        def ln_Ac(tg, c):
            c0, c1 = tg * 384, (tg + 1) * 384
            S.add("act", lambda e: e.activation(out=yb[:, c, :], in_=yT[:, c, c0:c1], func=AF.Copy),
                  reads=ykeys, writes=[("yb", c)])
            S.add("act", lambda e: e.activation(out=ysq[:, c, :], in_=yT[:, c, c0:c1], func=AF.Square),
                  reads=ykeys, writes=[("ysq", c)])

        def ln_A(tg):
            for c in range(8):
                ln_Ac(tg, c)

        def ln_B(tg):
            i1, i2 = 4, 5
            b1, b2 = PB[i1], PB[i2]

            def mms(e, bank, src):
                ins = None
                for c in range(8):
                    ins = e.matmul(bank[:, 0:384], ones[:], src[:, c, :], start=(c == 0), stop=(c == 7))
                return ins
            S.add("pe", lambda e: mms(e, b1, yb), reads=[("yb", c) for c in range(8)] + ["ones"],
                  writes=[("pb", i1)])
            S.add("pe", lambda e: mms(e, b2, ysq), reads=[("ysq", c) for c in range(8)] + ["ones"],
                  writes=[("pb", i2)])
            S.add("act", lambda e: e.activation(out=mean[:], in_=b1[:, 0:384], func=AF.Identity, scale=1.0 / 1024),
                  reads=[("pb", i1)], writes=["mean"])
            S.add("dve", lambda e: e.tensor_tensor(out=msq[:], in0=mean[:], in1=mean[:], op=ALU.mult),
                  reads=["mean"], writes=["msq"])
            S.add("dve", lambda e: e.scalar_tensor_tensor(out=rstd[:], in0=b2[:, 0:384], scalar=1.0 / 1024,
                                                          in1=msq[:], op0=ALU.mult, op1=ALU.subtract),
                  reads=[("pb", i2), "msq"], writes=["rstd"])
            S.add("act", lambda e: e.activation(out=rstd[:], in_=rstd[:], func=AF.Sqrt, bias=1e-5),
                  reads=["rstd"], writes=["rstd"])
            S.add("dve", lambda e: e.reciprocal(rstd[:], rstd[:]), reads=["rstd"], writes=["rstd"])

        def ln_C(tg, c):
            c0, c1 = tg * 384, (tg + 1) * 384
            tt = t1[c % 2]
            S.add("dve", lambda e: e.tensor_tensor(out=tt[:], in0=yT[:, c, c0:c1], in1=mean[:], op=ALU.subtract),
                  reads=ykeys + ["mean"], writes=[("t1", c % 2)])
            S.add("dve", lambda e: e.tensor_tensor(out=tt[:], in0=tt[:], in1=rstd[:], op=ALU.mult),
                  reads=[("t1", c % 2), "rstd"], writes=[("t1", c % 2)])
            S.add("act", lambda e: e.activation(
                out=AT[:, 8 + c, c0:c1], in_=tt[:], func=AF.Silu, scale=convp[:, c, 32:33],
                bias=convp[:, c, 33:34]),
                reads=[("t1", c % 2), "convp"], writes=[("AT", 8 + c, 3 * tg + i) for i in range(3)])

        ln_Ac(0, 0)
        ln_Ac(0, 1)
        for c in range(2, 8):
            ln_steps.append(lambda c=c: ln_Ac(0, c))
        for tg in range(3):
            ln_steps.append(lambda tg=tg: ln_B(tg))
            for c in range(8):
                ln_steps.append(lambda tg=tg, c=c: ln_C(tg, c))
                if tg < 2:
                    ln_steps.append(lambda tg=tg, c=c: ln_Ac(tg + 1, c))

        cur[0] = QKB
        QT = alloc("QT", [128, 2, 9, 512], BF16)
        KT = alloc("KT", [128, 2, 1280], BF16)
        Vd = alloc("Vd", [128, 10, 4, 128], BF16)
        kvf = [alloc("kvf%d" % i, [128, 512], F32) for i in range(2)]
        qb = [alloc("qb%d" % i, [128, 512], BF16, at=offs["uf"] + i * 1024) for i in range(3)]
        rt = [alloc("rt%d" % i, [128, 4, 64], F32, at=offs["uf"] + 3072 + i * 1024) for i in range(2)]
        qcnt = [0]

        def rope_ops(k, psrc, nh, b, dsts, dkeys, perm=False, kb=0):
            r = rt[k][:].rearrange("p a b -> p (a b)").rearrange("p (h a d) -> p h a d", a=2, d=16)[:, 0:nh]
            for a_ in range(2):
                tab = ropecs[:, b, a_, :].unsqueeze(1).to_broadcast([128, nh, 16])
                S.add("dve", lambda e, a_=a_, tab=tab: e.tensor_tensor(out=r[:, :, a_, :], in0=psrc[:, :, 0:16],
                                                                      in1=tab, op=ALU.mult),
                      reads=[("pb", kb), "ropecs", ("actrd", k)], writes=[("rt", k), "ufall"])
            if perm:
                rv = r.rearrange("p (u i) a d -> p u i a d", u=2)
                ra = [rv[:, :, :, 0, 0:8], rv[:, :, :, 0, 8:16]]
                rb = [rv[:, :, :, 1, 8:16], rv[:, :, :, 1, 0:8]]
            else:
                ra = [r[:, :, 0, 0:8], r[:, :, 0, 8:16]]
                rb = [r[:, :, 1, 8:16], r[:, :, 1, 0:8]]
            for dst, dk_ in zip(dsts, dkeys):
                if perm:
                    dv = dst.rearrange("p (i u d) -> p u i d", u=2, d=64)
                    do = [dv[:, :, :, 0:8], dv[:, :, :, 8:16]]
                else:
                    dv = dst.rearrange("p (h d) -> p h d", d=64)
                    do = [dv[:, :, 0:8], dv[:, :, 8:16]]
                for j in range(2):
                    S.add("dve", lambda e, j=j, do=do: e.tensor_tensor(out=do[j], in0=ra[j], in1=rb[j], op=ALU.add),
                          reads=[("rt", k)], writes=[dk_])

        def qkv_s1(ti, b):
            i_ = qcnt[0]
            qcnt[0] += 1
            k = i_ % 2
            k3 = i_ % 3
            Wt = W[wcur()]
            wk = wkeys(wcur())
            kb = (0, 1, 6, 7)[i_ % 4]
            bank = PB[kb]
            stg = kvf[k]

            def mm(e):
                ins = None
                for kc in range(16):
                    ins = e.matmul(bank[:], xT[:, kc, b * 128:(b + 1) * 128], Wt[:, kc, :],
                                   start=(kc == 0), stop=(kc == 15))
                return ins
            S.add("pe", mm, reads=wk + [("xT", b)], writes=[("pb", kb)])
            return (ti, b, k, k3, bank, stg, kb)

        def qkv_s1b(ctx):
            ti, b, k, k3, bank, stg, kb = ctx
            if ti < 2:
                S.add("act", lambda e: e.activation(
                    out=qb[k3][:].rearrange("p (i u d) -> p u i d", u=2, d=64),
                    in_=bank[:].rearrange("p (u i d) -> p u i d", u=2, d=64), func=AF.Copy),
                    reads=[("pb", kb)], writes=[("qb", k3), "ufall", ("actrd", k)])
                rope_ops(k, bank[:].rearrange("p (h d) -> p h d", d=64), 8, b, [qb[k3][:]], [("qb", k3)], perm=True, kb=kb)
            else:
                outp = b in (8, 9)
                S.add("act", lambda e: e.activation(out=qb[k3][:, 0:256], in_=bank[:, 0:256], func=AF.Copy),
                      reads=[("pb", kb)], writes=[("qb", k3), "ufall", ("actrd", k)])
                for u in range(2):
                    S.add("act", lambda e, u=u: e.activation(
                        out=Vd[:, b, :, :].rearrange("p h (u d) -> p h u d", u=2)[:, :, u, :],
                        in_=bank[:, 256:512].rearrange("p (h d) -> p h d", d=64), func=AF.Copy),
                        reads=[("pb", kb)], writes=[("Vd", b), ("actrd", k)])
                dsts = [qb[k3][:, 0:256]]
                dkeys = [("qb", k3)]
                if outp:
                    S.add("act", lambda e: e.activation(out=stg[:], in_=bank[:], func=AF.Copy),
                          reads=[("pb", kb)], writes=[("kvf", k), ("actrd", k)])
                    dsts.append(stg[:, 0:256])
                    dkeys.append(("kvf", k))
                rope_ops(k, bank[:, 0:256].rearrange("p (h d) -> p h d", d=64), 4, b, dsts, dkeys, kb=kb)
                if b == 8:
                    dma("sp", nkp_o, stg[:, 0:256], reads=[("kvf", k)])
                    dma("sp", nvp_o, stg[:, 256:512], reads=[("kvf", k)])
                if b == 9:
                    for s_ in range(16):
                        dma("sp", nks_o[s_, 120:128, :], stg[s_ * 8:(s_ + 1) * 8, 0:256], reads=[("kvf", k)])
                        dma("sp", nvs_o[s_, 120:128, :], stg[s_ * 8:(s_ + 1) * 8, 256:512], reads=[("kvf", k)])

        def qkv_s2(ctx):
            ti, b, k, k3, bank, stg, kb = ctx
            tb = 2 + k
            n = 4 if ti < 2 else 2

            def tr(e):
                ins = None
                for i in range(n):
                    ins = e.transpose(pbf(tb)[:, i, :], qb[k3][:, i * 128:(i + 1) * 128], ident[:])
                return ins
            S.add("pe", tr, reads=[("qb", k3), "ident"], writes=[("pb", tb)])
            if ti < 2:
                if b < 9:
                    S.add("act", lambda e: e.activation(
                        out=QT[:, ti, b - 1, :].rearrange("p (i t) -> p i t", i=4), in_=pbf(tb), func=AF.Copy),
                        reads=[("pb", tb)], writes=[("QT", ti, b)])
                else:
                    S.add("act", lambda e: e.activation(
                        out=QT[:, ti, 8, :].rearrange("p (s i t) -> p i s t", i=4, t=8),
                        in_=pbf(tb).rearrange("p i (s t) -> p i s t", t=8), func=AF.Copy),
                        reads=[("pb", tb)], writes=[("QT", ti, b)])
            else:
                S.add("act", lambda e: e.activation(out=KT[:, :, b * 128:(b + 1) * 128], in_=pbf(tb)[:, 0:2, :],
                                                    func=AF.Copy),
                      reads=[("pb", tb)], writes=[("KT", b)])

        pend = []
        qitems = [(ti, b) for ti in (2, 0, 1) for b in range(0 if ti == 2 else 1, 10)]
        nq = len(qitems)
        nl = len(ln_steps)
        li = 0
        for qi, (ti, b) in enumerate(qitems):
            ctx = qkv_s1(ti, b)
            if len(pend) >= 2:
                qkv_s2(pend.pop(0))
            tgt = (qi + 1) * nl // nq
            while li < tgt:
                ln_steps[li]()
                li += 1
            qkv_s1b(ctx)
            pend.append(ctx)
            if qi + 1 == nq or qitems[qi + 1][0] != ti:
                wdone()
        while li < nl:
            ln_steps[li]()
            li += 1
        MB = [alloc("MB%d" % m, [128, 512], BF16, at=(offs["xnb0"] + m * 1024) if m < 4 else offs["xnb1"])
              for m in range(5)]
        for m in range(5):
            if m < 3:
                ov = MB[m][:].rearrange("p (i t) -> p i t", i=4)
                iv = masks[:, m, :].unsqueeze(1).to_broadcast([128, 4, 128])
            elif m == 3:
                ov = MB[m][:].rearrange("p (s i t) -> p s i t", i=4, t=8)
                iv = masks[:, 3, :].rearrange("p (s t) -> p s t", t=8).unsqueeze(2).to_broadcast([128, 16, 4, 8])
            else:
                ov = MB[m][:].rearrange("p (a t) -> p a t", t=8)
                iv = masks[:, 4, 0:8].unsqueeze(1).to_broadcast([128, 64, 8])
            S.add("dve", lambda e, ov=ov, iv=iv: e.tensor_scalar(ov, iv, 30000.0, -30000.0, op0=ALU.mult,
                                                                 op1=ALU.add),
                  reads=["masks"], writes=[("MB", m), ("xnb", 0), ("xnb", 1), ("t1", 0), ("t1", 1), ("uo", 0), ("uo", 1)])

        while pend:
            qkv_s2(pend.pop(0))

        S.barrier()
        ralloc_reset()
        KcT = alloc("KcT", [128, 16, 2, 128], BF16)
        Vcd = alloc("Vcd", [128, 16, 4, 128], BF16)
        PT = [[alloc("PT%d_%d" % (i, j), [128, 512], BF16) for j in range(2)] for i in range(2)]
        den = [alloc("den%d" % i, [128, 512], F32) for i in range(2)]
        rscr = alloc("rscr", [128, 512], F32)
        S.add("pool", lambda e: e.memset(rscr[:], -1.0), writes=["rscr"])
        dma("pool", KcT[:].rearrange("p a b c -> p a (b c)"), ckT.rearrange("p (a x) -> p a x", a=16),
            writes=["KcT"], maxlast=2048)
        dma("pool", Vcd[:].rearrange("p s h d -> p s (h d)"), cvK.rearrange("p (s x) -> p s x", s=16),
            writes=[("Vcd", 0), ("Vcd", 1)], maxlast=2048)

        acnt = [0]

        def attn_A(qbk, kvh):
            k = acnt[0] % 2
            acnt[0] += 1
            jp, h0 = kvh // 2, (kvh % 2) * 64
            tq = (qbk - 1) * 128
            smp = (qbk == 9)
            qrhs = QT[h0:h0 + 64, jp, qbk - 1, :]
            qkeys = [("QT", jp, qbk)]
            sP, sC, bO, bD = PB[k], PB[2 + k], PB[4 + k], PB[6 + k]
            ptP, ptC = PT[k][0], PT[k][1]
            mC = 3 if smp else 0
            mP = 4 if smp else (2 if qbk == 1 else 1)

            pe_mask = (MASK_MODE == "pe")
            pe_mask_c = MASK_MODE in ("pe", "hybrid")

            def qk_c(e):
                if pe_mask_c:
                    e.matmul(sC[:], ident[:], MB[mC][:], start=True, stop=False)
                return e.matmul(sC[:], KT[h0:h0 + 64, jp, qbk * 128:(qbk + 1) * 128], qrhs, start=not pe_mask_c,
                                stop=True)
            S.add("pe", qk_c, reads=qkeys + [("KT", qbk), ("MB", mC), "ident"], writes=[("pb", 2 + k)])
            if not smp:
                def qk_p(e):
                    if pe_mask:
                        e.matmul(sP[:], ident[:], MB[mP][:], start=True, stop=False)
                    return e.matmul(sP[:], KT[h0:h0 + 64, jp, (qbk - 1) * 128:qbk * 128], qrhs, start=not pe_mask,
                                    stop=True)
                S.add("pe", qk_p, reads=qkeys + [("KT", qbk - 1), ("MB", mP), "ident"], writes=[("pb", k)])
            else:
                def qk_p(e):
                    ins = None
                    if pe_mask:
                        ins = e.matmul(sP[:], ident[:], MB[mP][:], start=True, stop=False)
                    for s_ in range(16):
                        ins = e.matmul(sP[:, s_ * 32:(s_ + 1) * 32], KcT[h0:h0 + 64, s_, jp, :],
                                       QT[h0:h0 + 64, jp, 8, s_ * 32:(s_ + 1) * 32], start=not pe_mask,
                                       stop=(s_ == 15) or not pe_mask)
                    return ins
                S.add("pe", qk_p, reads=qkeys + ["KcT", ("MB", mP), "ident"], writes=[("pb", k)])
            S.add("act", lambda e: e.activation(out=ptC[:], in_=sC[:], func=AF.Exp, scale=0.125),
                  reads=[("pb", 2 + k)], writes=[("PT", k, 1)])
            S.add("act", lambda e: e.activation(out=ptP[:], in_=sP[:], func=AF.Exp, scale=0.125),
                  reads=[("pb", k)], writes=[("PT", k, 0)])
            if not pe_mask:
                if not smp:
                    vC = ptC[:].rearrange("p (i t) -> p i t", i=4)
                    vP = ptP[:].rearrange("p (i t) -> p i t", i=4)
                    mCv = masks[:, mC, :].unsqueeze(1).to_broadcast([128, 4, 128])
                    mPv = masks[:, mP, :].unsqueeze(1).to_broadcast([128, 4, 128])
                else:
                    vC = ptC[:].rearrange("p (s i t) -> p s i t", i=4, t=8)
                    vP = ptP[:].rearrange("p (a t) -> p a t", t=8)
                    mCv = masks[:, 3, :].rearrange("p (s t) -> p s t", t=8).unsqueeze(2).to_broadcast(
                        [128, 16, 4, 8])
                    mPv = masks[:, 4, 0:8].unsqueeze(1).to_broadcast([128, 64, 8])
                if not pe_mask_c:
                    S.add("pool", lambda e: e.tensor_tensor(out=vC, in0=vC, in1=mCv, op=ALU.mult),
                          reads=[("PT", k, 1), "masks"], writes=[("PT", k, 1)])
                S.add("pool", lambda e: e.tensor_tensor(out=vP, in0=vP, in1=mPv, op=ALU.mult),
                      reads=[("PT", k, 0), "masks"], writes=[("PT", k, 0)])
            return (qbk, kvh, k, tq, bO, bD, ptP, ptC)

        def attn_B(ctx):
            qbk, kvh, k, tq, bO, bD, ptP, ptC = ctx
            smp = (qbk == 9)
            if not smp:
                def pv(e):
                    e.matmul(bO[:], Vd[:, qbk, kvh, :], ptC[:], start=True, stop=False)
                    return e.matmul(bO[:], Vd[:, qbk - 1, kvh, :], ptP[:], start=False, stop=True)
                S.add("pe", pv, reads=[("PT", k, 0), ("PT", k, 1), ("Vd", qbk), ("Vd", qbk - 1)],
                      writes=[("pb", 4 + k)])
            else:
                def pv(e):
                    ins = e.matmul(bO[:], Vd[:, 9, kvh, :], ptC[:], start=True, stop=False)
                    for s_ in range(16):
                        ins = e.matmul(bO[:, s_ * 32:(s_ + 1) * 32], Vcd[:, s_, kvh, :],
                                       ptP[:, s_ * 32:(s_ + 1) * 32], start=False, stop=(s_ == 15))
                    return ins
                S.add("pe", pv, reads=[("PT", k, 0), ("PT", k, 1), ("Vd", 9), ("Vcd", 0), ("Vcd", 1)],
                      writes=[("pb", 4 + k)])

            def dn(e):
                e.matmul(bD[:], ones[:], ptC[:], start=True, stop=False)
                return e.matmul(bD[:], ones[:], ptP[:], start=False, stop=True)
            S.add("pe", dn, reads=[("PT", k, 0), ("PT", k, 1), "ones"], writes=[("pb", 6 + k)])
            dk = den[k]
            dkeys = [("den", k)]
            if not smp:
                bDv = bD[:].rearrange("p (i t) -> p i t", i=4)
                bOv = bO[:].rearrange("p (i t) -> p i t", i=4)
                dv = dk[:, 0:256].rearrange("p (i t) -> p i t", i=2)
                adds = [(dv[hs * 64:(hs + 1) * 64], bDv[hs * 64:(hs + 1) * 64, hs:4:2, :],
                         esink[hs * 64:(hs + 1) * 64, 4 * kvh + hs:4 * kvh + 4:2].unsqueeze(2).to_broadcast(
                             [64, 2, 128])) for hs in range(2)]
                outs = [AT[hs * 64:(hs + 1) * 64, 2 * kvh:2 * kvh + 2, tq:tq + 128] for hs in range(2)]
                in0s = [bOv[hs * 64:(hs + 1) * 64, hs:4:2, :] for hs in range(2)]
                in1s = [dv[hs * 64:(hs + 1) * 64] for hs in range(2)]
            else:
                bD4 = bD[:].rearrange("p (s i t) -> p s i t", i=4, t=8)
                dv4 = dk[:, 0:256].rearrange("p (s i t) -> p s i t", i=2, t=8)
                adds = [(dv4[hs * 64:(hs + 1) * 64], bD4[hs * 64:(hs + 1) * 64, :, hs:4:2, :],
                         esink[hs * 64:(hs + 1) * 64, 4 * kvh + hs:4 * kvh + 4:2].unsqueeze(1).unsqueeze(
                             3).to_broadcast([64, 16, 2, 8])) for hs in range(2)]
                bOp = bO[:].rearrange("p (s i t) -> p i s t", i=4, t=8)
                dvp = dk[:, 0:256].rearrange("p (s i t) -> p i s t", i=2, t=8)
                outs = [AT[hs * 64:(hs + 1) * 64, 2 * kvh:2 * kvh + 2, tq:tq + 128].rearrange(
                    "p c (s t) -> p c s t", t=8) for hs in range(2)]
                in0s = [bOp[hs * 64:(hs + 1) * 64, hs:4:2, :, :] for hs in range(2)]
                in1s = [dvp[hs * 64:(hs + 1) * 64] for hs in range(2)]
            for hs in range(2):
                S.add("dve", lambda e, hs=hs: e.tensor_tensor(out=adds[hs][0], in0=adds[hs][1], in1=adds[hs][2],
                                                              op=ALU.add),
                      reads=[("pb", 6 + k), "esink"], writes=dkeys)
            S.add("act", lambda e: e.activation(out=dk[:, 0:256], in_=dk[:, 0:256], func=AF.Ln), reads=dkeys,
                  writes=dkeys)
            S.add("act", lambda e: e.activation(out=dk[:, 0:256], in_=dk[:, 0:256], func=AF.Exp, scale=-1.0),
                  reads=dkeys, writes=dkeys)
            for hs in range(2):
                S.add("dve", lambda e, hs=hs: e.tensor_tensor(out=outs[hs], in0=in0s[hs], in1=in1s[hs],
                                                              op=ALU.mult),
                      reads=[("pb", 4 + k)] + dkeys,
                      writes=[("AT", 2 * kvh, qbk - 1), ("AT", 2 * kvh + 1, qbk - 1)])

        items = [(qbk, kvh) for qbk in range(1, 10) for kvh in range(4)]
        prev_ctx = None
        for it in items:
            ctx = attn_A(*it)
            if prev_ctx is not None:
                attn_B(prev_ctx)
            prev_ctx = ctx
        attn_B(prev_ctx)

        S.barrier()
        ralloc_reset()
        h = alloc("h", [128, 9, D], F32)
        rtmp = [alloc("rtmp%d" % i, [128, 384], F32, at=offs["xnb%d" % i]) for i in range(2)]
        xnb[2] = alloc("xnb2b", [128, D], BF16)
        dma("sp", gbc[:], gains_d[1].partition_broadcast(128), writes=["gbc"])
        for dg in range(4):
            for b in range(1, 10):
                dma("sp", h[:, b - 1, dg * 512:(dg + 1) * 512], xin[b * 128:(b + 1) * 128, dg * 512:(dg + 1) * 512],
                    writes=[("h", b, dg)])
        ocnt = [0]
        for dg in range(4):
            Wt = W[wcur()]
            wk = wkeys(wcur())
            for b in range(1, 10):
                k = ocnt[0] % 2
                ocnt[0] += 1
                bank = PB[k]

                def mm(e, b=b, bank=bank, Wt=Wt):
                    ins = None
                    for c in range(16):
                        ins = e.matmul(bank[:], AT[:, c, (b - 1) * 128:b * 128], Wt[:, c, :],
                                       start=(c == 0), stop=(c == 15))
                    return ins
                S.add("pe", mm, reads=wk + [("AT", c, b - 1) for c in range(16)], writes=[("pb", k)])
                hv = h[:, b - 1, dg * 512:(dg + 1) * 512]
                S.add("dve", lambda e, hv=hv, bank=bank: e.tensor_tensor(out=hv, in0=hv, in1=bank[:], op=ALU.add),
                      reads=[("pb", k), ("h", b, dg)], writes=[("h", b, dg)])
                if dg == 3:
                    hsrc = lambda bb: (h[:, bb - 1, :], [("h", bb, i) for i in range(4)], bb, 20 + 2 * bb)
                    if b >= 4:
                        norm_tr(b - 3, (6, 7))
                    norm_act(*hsrc(b))
                    if b >= 2:
                        norm_dve(*hsrc(b - 1))
            if dg == 3:
                norm_dve(*hsrc(9))
                norm_tr(7, (6, 7))
                norm_tr(8, (6, 7))
                norm_tr(9, (6, 7))
            wdone()

        def fin_act(b):
            sc = 40 + 2 * b
            ms = stat[:, sc:sc + 1]
            rs = stat[:, sc + 1:sc + 2]
            hb = h[:, b - 1, :]
            hk = [("h", b, i) for i in range(4)]
            kx = b % 2
            S.add("act", lambda e: e.activation(out=xnb[kx][:], in_=hb, func=AF.Square,
                                                scale=float(1.0 / np.sqrt(D)), accum_out=ms),
                  reads=hk, writes=[("xnb", kx), ("st", sc)])
            S.add("act", lambda e: e.activation(out=rs, in_=ms, func=AF.Sqrt, bias=1e-6),
                  reads=[("st", sc)], writes=[("st", sc + 1)])

        def fin_dve(b):
            sc = 40 + 2 * b
            rs = stat[:, sc + 1:sc + 2]
            hb = h[:, b - 1, :]
            hk = [("h", b, i) for i in range(4)]
            S.add("dve", lambda e: e.reciprocal(rs, rs), reads=[("st", sc + 1)], writes=[("st", sc + 1)])
            S.add("dve", lambda e: e.scalar_tensor_tensor(out=hb, in0=hb, scalar=rs, in1=gbc[:], op0=ALU.mult,
                                                          op1=ALU.mult),
                  reads=hk + [("st", sc + 1), "gbc"], writes=hk)
            dma("sp", y_o[(b - 1) * 128:b * 128, :], hb, reads=hk)

        ucnt = [0]
        dcnt = [0]
        dma("sp", nks_o[:, 0:120, :], ck_raw[:, 8:128, :])
        dma("sp", nvs_o[:, 0:120, :], cv_raw[:, 8:128, :])
        dma("sp", ncs_o[:, 0:22, :], st_raw[:, 8:30, :])
        for s_ in range(4):
            for ft in range(4):
                Wt = W[wcur()]
                wk = wkeys(wcur())
                for fc in range(4):
                    fcg = ft * 4 + fc
                    k = ucnt[0] % 2
                    ucnt[0] += 1
                    banks = [PB[3 * k + i] for i in range(3)]

                    def mm(e, Wt=Wt, fc=fc, banks=banks):
                        ins = None
                        for kc in range(16):
                            for tg in range(3):
                                ins = e.matmul(banks[tg][:, 0:384], Wt[:, kc, fc * 128:(fc + 1) * 128],
                                               xT[:, kc, 128 + tg * 384:128 + (tg + 1) * 384],
                                               start=(kc == 0), stop=(kc == 15))
                        return ins
                    S.add("pe", mm, reads=wk + [("xT", b) for b in range(1, 10)],
                          writes=[("pb", 3 * k + i) for i in range(3)])
                    for tg in range(3):
                        r = rtmp[tg % 2]
                        S.add("act", lambda e, r=r, bank=banks[tg]: e.activation(out=r[:], in_=bank[:, 0:384],
                                                                                func=AF.Relu),
                              reads=[("pb", 3 * k + tg)], writes=[("rtmp", tg % 2), ("xnb", tg % 2)])
                        S.add("dve", lambda e, r=r, fcg=fcg, tg=tg: e.tensor_tensor(
                            out=AT[:, fcg, tg * 384:(tg + 1) * 384], in0=r[:], in1=r[:], op=ALU.mult),
                            reads=[("rtmp", tg % 2), ("xnb", tg % 2)],
                            writes=[("AT", fcg, 3 * tg + i) for i in range(3)])
                wdone()
            for dg in range(4):
                Wt = W[wcur()]
                wk = wkeys(wcur())
                last = (s_ == 3)
                if last and dg == 0:
                    dma("sp", gbc[:], gains_d[2].partition_broadcast(128), writes=["gbc"])
                for b in range(1, 10):
                    k = dcnt[0] % 2
                    dcnt[0] += 1
                    bank = PB[6 + k]

                    def mm(e, b=b, bank=bank, Wt=Wt):
                        ins = None
                        for c in range(16):
                            ins = e.matmul(bank[:], AT[:, c, (b - 1) * 128:b * 128], Wt[:, c, :],
                                           start=(c == 0), stop=(c == 15))
                        return ins
                    S.add("pe", mm, reads=wk + [("AT", c, b - 1) for c in range(16)], writes=[("pb", 6 + k)])
                    hv = h[:, b - 1, dg * 512:(dg + 1) * 512]
                    S.add("dve", lambda e, hv=hv, bank=bank: e.tensor_tensor(out=hv, in0=hv, in1=bank[:], op=ALU.add),
                          reads=[("pb", 6 + k), ("h", b, dg)], writes=[("h", b, dg)])
                    if last and dg == 3:
                        fin_act(b)
                        if b >= 2:
                            fin_dve(b - 1)
                if last and dg == 3:
                    fin_dve(9)
                wdone()

        S.emit(nc, es)
    return nc


_NC_CACHE = {}


def _consts(hf):
    f32 = np.float32
    pos = np.zeros((10, 128), dtype=f32)
    i = np.arange(128)
    pos[0] = (i - 112) if hf == 0 else (912 + i)
    for b in range(1, 9):
        pos[b] = 16 + hf * 1024 + (b - 1) * 128 + i
    pos[9] = 8192 + (i % 8)
    inv = np.power(f32(500000.0), -np.arange(8, dtype=f32) * f32(2.0) / f32(16)).astype(f32)
    ang = (pos[:, :, None].astype(f32) * inv[None, None, :]).astype(f32)
    co, si = np.cos(ang).astype(f32), np.sin(ang).astype(f32)
    cs = np.stack([np.concatenate([co, co], axis=-1), np.concatenate([si, -si], axis=-1)], axis=2)
    ropecs = np.ascontiguousarray(cs.transpose(1, 0, 2, 3)).reshape(128, 320).astype(f32)
    j = np.arange(128)[:, None]
    q = np.arange(128)[None, :]
    m = np.zeros((128, 5, 128), dtype=f32)
    m[:, 0] = (j <= q)
    m[:, 1] = (j > q)
    m[:, 2] = (j > q) & ((j >= 112) if hf == 0 else True)
    m[:, 3] = ((j // 8) == (q // 8)) & ((j % 8) <= (q % 8))
    m[:, 4] = (j > (q % 8))
    return ropecs, m.reshape(128, 640)


def kernel(x_prompt, x_sample, cache_k, cache_v, state_conv, meta_tokens, norm_mix, w_in, attn_sinks,
           w_dw, b_dw, conv_ln_g, conv_ln_b, w_out, norm_mlp, w_up, w_down, norm_final):
    f32 = np.float32
    x_prompt = np.asarray(x_prompt, f32); x_sample = np.asarray(x_sample, f32)
    cache_k = np.asarray(cache_k, f32); cache_v = np.asarray(cache_v, f32)
    state_conv = np.asarray(state_conv, f32)
    if "nc" not in _NC_CACHE:
        _NC_CACHE["nc"] = build_nc()
    nc = _NC_CACHE["nc"]

    w_in_ = np.ascontiguousarray(np.asarray(w_in, f32)[0])
    w_out_ = np.ascontiguousarray(np.asarray(w_out, f32)[0])
    w_up_ = np.ascontiguousarray(np.asarray(w_up, f32)[0])
    w_down_ = np.ascontiguousarray(np.asarray(w_down, f32)[0])
    gains = np.stack([np.asarray(norm_mix, f32)[0], np.asarray(norm_mlp, f32)[0],
                      np.asarray(norm_final, f32)]).astype(f32)
    sinks = np.ascontiguousarray(np.asarray(attn_sinks, f32)[0])
    cp = np.concatenate([np.asarray(w_dw, f32)[0], np.asarray(b_dw, f32), np.asarray(conv_ln_g, f32),
                         np.asarray(conv_ln_b, f32)], axis=0)
    convp = np.ascontiguousarray(cp.reshape(34, 8, 128).transpose(2, 1, 0)).reshape(128, 8 * 34)
    meta = np.asarray(meta_tokens, f32)
    cst = [_consts(0), _consts(1)]

    in_maps = []
    for c in range(NCORES):
        n, hf = c // 2, c % 2
        xin = np.zeros((1280, D), dtype=f32)
        if hf == 0:
            xin[112:128] = meta
        else:
            xin[0:128] = x_prompt[n, 896:1024]
        xin[128:1152] = x_prompt[n, hf * 1024:(hf + 1) * 1024]
        xin[1152:1280] = x_sample[16 * c:16 * (c + 1)].reshape(128, D)
        ck = cache_k[0, 16 * c:16 * (c + 1)]
        cv = cache_v[0, 16 * c:16 * (c + 1)]
        ckT = np.ascontiguousarray(ck.reshape(16, 128, 2, 2, 64).transpose(3, 4, 0, 2, 1)).reshape(128, 16 * 2 * 128)
        cvK = cv.reshape(16, 128, 4, 1, 64).transpose(1, 0, 2, 3, 4)
        cvK = np.ascontiguousarray(np.broadcast_to(cvK, (128, 16, 4, 2, 64))).reshape(128, 16 * 4 * 2 * 64)
        st = state_conv[0, 16 * c:16 * (c + 1)]
        stT = np.ascontiguousarray(st.reshape(16, 30, 8, 128).transpose(3, 2, 1, 0))
        in_maps.append({
            "xin": xin, "ckT": ckT, "cvK": cvK,
            "ck_raw": np.ascontiguousarray(ck.reshape(16, 128, 256)),
            "cv_raw": np.ascontiguousarray(cv.reshape(16, 128, 256)),
            "stT": stT, "st_raw": np.ascontiguousarray(st),
            "masks": cst[hf][1], "ropecs": cst[hf][0], "convp": convp, "gains": gains, "sinks": sinks,
            "w_in": w_in_, "w_out": w_out_, "w_up": w_up_, "w_down": w_down_,
        })
    res = run_bass_kernel_spmd(nc, in_maps, core_ids=list(range(NCORES)))
    R = res.results

    y_prompt = np.zeros((4, 2048, D), f32)
    y_sample = np.zeros((128, 8, D), f32)
    nkp = np.zeros((1, 4, 128, 4, 64), f32)
    nvp = np.zeros((1, 4, 128, 4, 64), f32)
    ncp = np.zeros((1, 4, 30, 1024), f32)
    nks = np.zeros((1, 128, 128, 4, 64), f32)
    nvs = np.zeros((1, 128, 128, 4, 64), f32)
    ncs = np.zeros((1, 128, 30, 1024), f32)
    for c in range(NCORES):
        n, hf = c // 2, c % 2
        r = R[c]
        y_prompt[n, hf * 1024:(hf + 1) * 1024] = r["y"][0:1024]
        y_sample[16 * c:16 * (c + 1)] = r["y"][1024:1152].reshape(16, 8, D)
        if hf == 1:
            nkp[0, n] = r["nkp"].reshape(128, 4, 64)
            nvp[0, n] = r["nvp"].reshape(128, 4, 64)
            ncp[0, n] = r["ncp"][2:32]
        nks[0, 16 * c:16 * (c + 1)] = r["nks"].reshape(16, 128, 4, 64)
        nvs[0, 16 * c:16 * (c + 1)] = r["nvs"].reshape(16, 128, 4, 64)
        ncs[0, 16 * c:16 * (c + 1)] = r["ncs"]
    return (y_prompt, y_sample, nkp, nvp, ncp, nks, nvs, ncs)
```
